# Optimizing a Trainium2 kernel written in Bass

```python
import jax
import jax.numpy as jnp
from jax import lax
import numpy as np

D_MODEL = 1024
BATCH = 16
SEQ = 256
DEPTH = 4
DEC_BATCH = 4
DEC_SEQ = 2048
PAST_LEN = 256

GRID_W = 64
HEAD_DIM = 64
NA_WIDTH = D_MODEL // 4
NA_HEADS = NA_WIDTH // HEAD_DIM
NA_WIN_R = 8
NA_WIN_C = 16
MLA_WIDTH = D_MODEL // 2
MLA_V = 64
MLA_HEADS = MLA_WIDTH // MLA_V
MLA_NOPE = 64
MLA_ROPE = 32
MLA_Q_RANK = 3 * D_MODEL // 8
MLA_KV_RANK = D_MODEL // 4
SGU_WIDTH = D_MODEL // 4
SGU_GROUPS = 4
SGU_CHUNK = 128
D_FF = 11 * D_MODEL // 4
CONV_W = 3
ROPE_THETA = 10000.0
EPS = 1e-6
Q_BLOCK = 128
NEG_INF = -1e30
NA_SCALE = HEAD_DIM ** -0.5
MLA_SCALE = (MLA_NOPE + MLA_ROPE) ** -0.5
PROJ_SIZES = (NA_WIDTH, NA_WIDTH, NA_WIDTH, MLA_Q_RANK, MLA_KV_RANK, MLA_ROPE, 2 * SGU_WIDTH)
IN_WIDTH = sum(PROJ_SIZES)

kernel_name = 'hybrid_na_mla_sgu_diffusion_step'


def rmsnorm(x, g):
    xf = x.astype(jnp.float32)
    y = xf * lax.rsqrt(jnp.mean(xf * xf, axis=-1, keepdims=True) + EPS)
    return (y * g.astype(jnp.float32)).astype(x.dtype)


def split_heads(a, n):
    return a.reshape(a.shape[:-1] + (n, a.shape[-1] // n))


def merge_heads(a):
    return a.reshape(a.shape[:-2] + (-1,))


def axial_rope_tables(n_tokens):
    t = jnp.arange(n_tokens)
    n_freq = MLA_ROPE // 4
    inv_freq = ROPE_THETA ** (-jnp.arange(n_freq, dtype=jnp.float32) / n_freq)
    ang_r = (t // GRID_W).astype(jnp.float32)[:, None] * inv_freq
    ang_c = (t % GRID_W).astype(jnp.float32)[:, None] * inv_freq
    return (jnp.cos(ang_r), jnp.sin(ang_r), jnp.cos(ang_c), jnp.sin(ang_c))


def _rotate(x, cos, sin):
    x1, x2 = jnp.split(x, 2, axis=-1)
    cos = cos.astype(x.dtype)
    sin = sin.astype(x.dtype)
    return jnp.concatenate([x1 * cos - x2 * sin, x1 * sin + x2 * cos], axis=-1)


def axial_rope(x, tables):
    cos_r, sin_r, cos_c, sin_c = tables
    x_row, x_col = jnp.split(x, 2, axis=-1)
    return jnp.concatenate([_rotate(x_row, cos_r, sin_r), _rotate(x_col, cos_c, sin_c)], axis=-1)


def blocked_attention(q, k, v, scale):
    b, tq, h, dk = q.shape
    nb = tq // Q_BLOCK
    qb = q.reshape(b, nb, Q_BLOCK, h, dk).transpose(1, 0, 2, 3, 4)

    def one_block(q_blk):
        s = jnp.einsum('bqhd,bkhd->bhqk', q_blk, k).astype(jnp.float32) * scale
        p = jax.nn.softmax(s, axis=-1).astype(v.dtype)
        return jnp.einsum('bhqk,bkhd->bqhd', p, v)

    o = lax.map(one_block, qb)
    return o.transpose(1, 0, 2, 3, 4).reshape(b, tq, h, v.shape[-1])


def neighbourhood_attention(q, k, v, k_ctx, v_ctx, rpb, rows):
    b, t, h, dh = q.shape
    kr = min(NA_WIN_R, rows)
    qg = q.reshape(b, rows, GRID_W, h, dh)
    kg = k.reshape(b, rows, GRID_W, h, dh)
    vg = v.reshape(b, rows, GRID_W, h, dh)
    r = jnp.arange(rows)
    row_idx = jnp.clip(r - kr // 2, 0, rows - kr)[:, None] + jnp.arange(kr)[None, :]
    kb = kg[:, row_idx]
    vb = vg[:, row_idx]
    cols = jnp.arange(GRID_W)
    col_start = jnp.clip(cols - NA_WIN_C // 2, 0, GRID_W - NA_WIN_C)
    col_mask = (cols[None, :] >= col_start[:, None]) & (cols[None, :] < col_start[:, None] + NA_WIN_C)
    row_off = row_idx - r[:, None] + (NA_WIN_R - 1)
    col_off = jnp.clip(cols[None, :] - cols[:, None], -(NA_WIN_C - 1), NA_WIN_C - 1) + (NA_WIN_C - 1)
    bias = rpb[:, row_off[:, None, :, None], col_off[None, :, None, :]]
    s_loc = jnp.einsum('brqhd,brkchd->bhrqkc', qg, kb).astype(jnp.float32) * NA_SCALE
    s_loc = s_loc + bias.astype(jnp.float32)[None]
    s_loc = jnp.where(col_mask[None, None, None, :, None, :], s_loc, NEG_INF)
    n_loc = kr * GRID_W
    s_loc = s_loc.reshape(b, h, rows, GRID_W, n_loc)
    s_ctx = jnp.einsum('brqhd,bkhd->bhrqk', qg, k_ctx).astype(jnp.float32) * NA_SCALE
    p = jax.nn.softmax(jnp.concatenate([s_loc, s_ctx], axis=-1), axis=-1).astype(v.dtype)
    p_loc = p[..., :n_loc].reshape(b, h, rows, GRID_W, kr, GRID_W)
    p_ctx = p[..., n_loc:]
    o = jnp.einsum('bhrqkc,brkchd->brqhd', p_loc, vb) + jnp.einsum('bhrqk,bkhd->brqhd', p_ctx, v_ctx)
    return o.reshape(b, t, h * dh)


def spatial_gating(uv, g_sgu, w_s, b_s):
    u, v = jnp.split(jax.nn.gelu(uv), 2, axis=-1)
    v = rmsnorm(v, g_sgu)
    b, t, _ = v.shape
    vc = v.reshape(b, t // SGU_CHUNK, SGU_CHUNK, SGU_GROUPS, SGU_WIDTH // SGU_GROUPS)
    mixed = jnp.einsum('gpk,bnkgd->bnpgd', w_s, vc) + b_s.T[:, :, None]
    return u * mixed.reshape(b, t, SGU_WIDTH)


def conv_ffn(h, w_in, conv_w, conv_b, w_out):
    a = h @ w_in
    t = a.shape[1]
    ap = jnp.pad(a, ((0, 0), (CONV_W // 2, CONV_W // 2), (0, 0)))
    a = sum(ap[:, i:i + t] * conv_w[i] for i in range(CONV_W)) + conv_b
    gate, val = jnp.split(a, 2, axis=-1)
    return (jax.nn.silu(gate) * val) @ w_out


def modulation(cond, w_mod, b_mod):
    m = jax.nn.silu(cond) @ w_mod + b_mod
    return jnp.split(m[:, None, :], 6, axis=-1)


def _split_proj(z):
    offs = np.cumsum(PROJ_SIZES)[:-1].tolist()
    return jnp.split(z, offs, axis=-1)


def _mla_queries(cq, g_cq, w_uq):
    q = split_heads(rmsnorm(cq, g_cq) @ w_uq, MLA_HEADS)
    return q[..., :MLA_NOPE], q[..., MLA_NOPE:]


def _mla_attend(q_nope, q_rope, ckv, k_rope, w_ukv):
    kv = split_heads(ckv @ w_ukv, MLA_HEADS)
    k_nope, v = kv[..., :MLA_NOPE], kv[..., MLA_NOPE:]
    b, tk, h, _ = k_nope.shape
    k = jnp.concatenate([k_nope, jnp.broadcast_to(k_rope[:, :, None, :], (b, tk, h, MLA_ROPE))], axis=-1)
    q = jnp.concatenate([q_nope, q_rope], axis=-1)
    return merge_heads(blocked_attention(q, k, v, MLA_SCALE))


def context_mixer(h, lp):
    qa, ka, va, cq, ckv, krope, uv = _split_proj(h @ lp['w_in'])
    qa, ka, va = split_heads(qa, NA_HEADS), split_heads(ka, NA_HEADS), split_heads(va, NA_HEADS)
    ckv = rmsnorm(ckv, lp['g_ckv'])
    o_a = merge_heads(blocked_attention(qa, ka, va, NA_SCALE))
    q_nope, q_rope = _mla_queries(cq, lp['g_cq'], lp['w_uq'])
    o_b = _mla_attend(q_nope, q_rope, ckv, krope, lp['w_ukv'])
    o_c = spatial_gating(uv, lp['g_sgu'], lp['w_sgu'], lp['b_sgu'])
    return jnp.concatenate([o_a, o_b, o_c], axis=-1), (ka, va, ckv, krope)


def latent_mixer(h, lp, ctx, rows, rope_q, rope_k):
    ka_ctx, va_ctx, ckv_ctx, krope_ctx = ctx
    qa, ka, va, cq, ckv, krope, uv = _split_proj(h @ lp['w_in'])
    qa, ka, va = split_heads(qa, NA_HEADS), split_heads(ka, NA_HEADS), split_heads(va, NA_HEADS)
    o_a = neighbourhood_attention(qa, ka, va, ka_ctx, va_ctx, lp['na_rpb'], rows)
    ckv = rmsnorm(ckv, lp['g_ckv'])
    q_nope, q_rope = _mla_queries(cq, lp['g_cq'], lp['w_uq'])
    q_rope = axial_rope(q_rope, rope_q)
    krope = axial_rope(krope, rope_k)
    o_b = _mla_attend(q_nope, q_rope, jnp.concatenate([ckv_ctx, ckv], axis=1),
                      jnp.concatenate([krope_ctx, krope], axis=1), lp['w_ukv'])
    o_c = spatial_gating(uv, lp['g_sgu'], lp['w_sgu'], lp['b_sgu'])
    return jnp.concatenate([o_a, o_b, o_c], axis=-1), None


def residual_block(x, mods, lp, mixer):
    sh1, sc1, g1, sh2, sc2, g2 = mods
    h = rmsnorm(x, lp['g_mix']) * (1 + sc1) + sh1
    mix, ctx_tensors = mixer(h)
    x = x + g1 * (mix @ lp['w_out'])
    h = rmsnorm(x, lp['g_ffn']) * (1 + sc2) + sh2
    x = x + g2 * conv_ffn(h, lp['w_ffn_in'], lp['ffn_conv_w'], lp['ffn_conv_b'], lp['w_ffn_out'])
    return x, ctx_tensors


def setup_inputs(seed: int = 0) -> dict:
    key = jax.random.key(seed)
    ks = jax.random.split(key, 27)
    L = DEPTH

    def nrm(k, shape, s):
        return jax.random.normal(k, shape, jnp.float32) * s

    def gain(k, shape):
        return 1.0 + 0.05 * jax.random.normal(k, shape, jnp.float32)

    return {
        'x_prompt': nrm(ks[0], (BATCH, SEQ, D_MODEL), 1.0),
        'x_sample': nrm(ks[1], (DEC_BATCH, DEC_SEQ, D_MODEL), 1.0),
        'cache_na_k': nrm(ks[2], (DEC_BATCH, L, PAST_LEN, NA_HEADS, HEAD_DIM), 1.0),
        'cache_na_v': nrm(ks[3], (DEC_BATCH, L, PAST_LEN, NA_HEADS, HEAD_DIM), 1.0),
        'cache_mla_ckv': nrm(ks[4], (DEC_BATCH, L, PAST_LEN, MLA_KV_RANK), 1.0),
        'cache_mla_krope': nrm(ks[5], (DEC_BATCH, L, PAST_LEN, MLA_ROPE), 1.0),
        'c': nrm(ks[6], (DEC_BATCH, D_MODEL), 1.0),
        'c_ctx': nrm(ks[7], (D_MODEL,), 1.0),
        'w_mod': nrm(ks[8], (L, D_MODEL, 6 * D_MODEL), 0.5 * D_MODEL ** -0.5),
        'b_mod': nrm(ks[9], (L, 6 * D_MODEL), 0.02),
        'g_mix': gain(ks[10], (L, D_MODEL)),
        'w_in': nrm(ks[11], (L, D_MODEL, IN_WIDTH), D_MODEL ** -0.5),
        'na_rpb': nrm(ks[12], (L, NA_HEADS, 2 * NA_WIN_R - 1, 2 * NA_WIN_C - 1), 0.5),
        'g_cq': gain(ks[13], (L, MLA_Q_RANK)),
        'w_uq': nrm(ks[14], (L, MLA_Q_RANK, MLA_HEADS * (MLA_NOPE + MLA_ROPE)), MLA_Q_RANK ** -0.5),
        'g_ckv': gain(ks[15], (L, MLA_KV_RANK)),
        'w_ukv': nrm(ks[16], (L, MLA_KV_RANK, MLA_HEADS * (MLA_NOPE + MLA_V)), MLA_KV_RANK ** -0.5),
        'g_sgu': gain(ks[17], (L, SGU_WIDTH)),
        'w_sgu': nrm(ks[18], (L, SGU_GROUPS, SGU_CHUNK, SGU_CHUNK), SGU_CHUNK ** -0.5),
        'b_sgu': gain(ks[19], (L, SGU_GROUPS, SGU_CHUNK)),
        'w_out': nrm(ks[20], (L, D_MODEL, D_MODEL), D_MODEL ** -0.5),
        'g_ffn': gain(ks[21], (L, D_MODEL)),
        'w_ffn_in': nrm(ks[22], (L, D_MODEL, 2 * D_FF), D_MODEL ** -0.5),
        'ffn_conv_w': nrm(ks[23], (L, CONV_W, 2 * D_FF), CONV_W ** -0.5),
        'ffn_conv_b': nrm(ks[24], (L, 2 * D_FF), 0.02),
        'w_ffn_out': nrm(ks[25], (L, D_FF, D_MODEL), D_FF ** -0.5),
        'g_final': gain(ks[26], (D_MODEL,)),
    }


def reference(x_prompt, x_sample, cache_na_k, cache_na_v, cache_mla_ckv, cache_mla_krope, c, c_ctx,
              w_mod, b_mod, g_mix, w_in, na_rpb, g_cq, w_uq, g_ckv, w_ukv, g_sgu, w_sgu, b_sgu,
              w_out, g_ffn, w_ffn_in, ffn_conv_w, ffn_conv_b, w_ffn_out, g_final):
    rows = x_sample.shape[1] // GRID_W
    rope_k = axial_rope_tables(x_sample.shape[1])
    rope_q = tuple(tab[:, None, :] for tab in rope_k)
    xp, xs = x_prompt, x_sample
    new_k, new_v, new_ckv, new_kr = [], [], [], []
    for l in range(DEPTH):
        lp = {
            'g_mix': g_mix[l], 'w_in': w_in[l], 'na_rpb': na_rpb[l], 'g_cq': g_cq[l], 'w_uq': w_uq[l],
            'g_ckv': g_ckv[l], 'w_ukv': w_ukv[l], 'g_sgu': g_sgu[l], 'w_sgu': w_sgu[l], 'b_sgu': b_sgu[l],
            'w_out': w_out[l], 'g_ffn': g_ffn[l], 'w_ffn_in': w_ffn_in[l], 'ffn_conv_w': ffn_conv_w[l],
            'ffn_conv_b': ffn_conv_b[l], 'w_ffn_out': w_ffn_out[l],
        }
        mods_ctx = modulation(c_ctx[None, :], w_mod[l], b_mod[l])
        xp, (ka, va, ckv, kr) = residual_block(xp, mods_ctx, lp, lambda h: context_mixer(h, lp))
        new_k.append(ka)
        new_v.append(va)
        new_ckv.append(ckv)
        new_kr.append(kr)
        mods_lat = modulation(c, w_mod[l], b_mod[l])
        ctx_l = (cache_na_k[:, l], cache_na_v[:, l], cache_mla_ckv[:, l], cache_mla_krope[:, l])
        xs, _ = residual_block(xs, mods_lat, lp,
                               lambda h: latent_mixer(h, lp, ctx_l, rows, rope_q, rope_k))
    y_prompt = rmsnorm(xp, g_final)
    y_sample = rmsnorm(xs, g_final)
    new_na_k = jnp.stack(new_k, axis=1)
    new_na_v = jnp.stack(new_v, axis=1)
    new_mla_ckv = jnp.stack(new_ckv, axis=1)
    new_mla_krope = jnp.stack(new_kr, axis=1)
    return (y_prompt, y_sample, new_na_k, new_na_v, new_mla_ckv, new_mla_krope)
```

```python
import os
import numpy as np
import concourse.bass as bass
import concourse.mybir as mybir
from concourse.bass_utils import run_bass_kernel_spmd
from contextlib import ExitStack

F32 = mybir.dt.float32
BF16 = mybir.dt.bfloat16
ALU = mybir.AluOpType
AF = mybir.ActivationFunctionType
AX = mybir.AxisListType


class T:
    __slots__ = ("name", "w", "r", "excl")

    def __init__(self, name, excl=False):
        self.name = name
        self.w = []
        self.r = {}
        self.excl = excl


class Sched:
    EPOCH = 16000

    def __init__(self, nc, es, n_dma_slots=12, strict_same=True):
        self.nc = nc
        self.es = es
        self.eng = {"pe": nc.tensor, "act": nc.scalar, "dve": nc.vector, "pool": nc.gpsimd, "sp": nc.sync}
        self.cnt = {e: 0 for e in self.eng}
        self.sems = {e: [] for e in self.eng}
        self.waited = {e: {} for e in self.eng}
        self.strict_same = strict_same
        self.slots = []
        self.qslots = {}
        self.qrr = {}
        for q in ("sp", "pool", "act"):
            ids = []
            for i in range(n_dma_slots):
                sem = es.enter_context(nc.semaphore("d_%s_%d" % (q, i)))
                self.slots.append([sem, 0])
                ids.append(len(self.slots) - 1)
            self.qslots[q] = ids
            self.qrr[q] = 0
        self.nwaits = 0

    def _esem(self, e, seq):
        k = (seq - 1) // self.EPOCH
        while len(self.sems[e]) <= k:
            self.sems[e].append(self.es.enter_context(self.nc.semaphore("s_%s_%d" % (e, len(self.sems[e])))))
        return self.sems[e][k], (seq - 1) % self.EPOCH + 1

    def _wait(self, e, deps):
        need = {}
        for d in deps:
            if d is None:
                continue
            if d[0] == "e":
                _, e2, s2 = d
                if e2 == e and (e == "pe" or not self.strict_same):
                    continue
                key = e2
            else:
                _, sid, s2 = d
                key = ("d", sid)
            if self.waited[e].get(key, 0) >= s2:
                continue
            if need.get(key, 0) < s2:
                need[key] = s2
        for key, s2 in need.items():
            if isinstance(key, tuple):
                sem = self.slots[key[1]][0]
                self.eng[e].wait_ge(sem, 16 * s2)
            else:
                sem, val = self._esem(key, s2)
                self.eng[e].wait_ge(sem, val)
            self.waited[e][key] = s2
            self.nwaits += 1

    def _deps(self, reads, writes, dma=None):
        deps = []
        for t in reads:
            deps.extend(t.w)
        for t in writes:
            if dma and not t.r and t.w and all(x[0] == "d" and x[1] in self.qslots[dma] for x in t.w):
                continue
            deps.extend(t.w)
            deps.extend(t.r.values())
        return deps

    def op(self, e, fn, reads=(), writes=(), inc=True):
        deps = self._deps(reads, writes)
        for t in reads:
            if t.excl:
                deps.extend(v for k, v in t.r.items() if k != e)
        self._wait(e, deps)
        ins = fn(self.eng[e])
        seq = self.cnt[e] + 1
        if inc:
            self.cnt[e] = seq
            sem, _ = self._esem(e, seq)
            ins.then_inc(sem, 1)
        tag = ("e", e, seq)
        for t in reads:
            t.r[e] = tag
        for t in writes:
            t.w = [tag]
            t.r = {}
        return ins

    def dma(self, q, out, in_, reads=(), writes=(), **kw):
        ids = self.qslots[q]
        sid = ids[self.qrr[q]]
        self.qrr[q] = (self.qrr[q] + 1) % len(ids)
        slot = self.slots[sid]
        deps = self._deps(reads, writes, dma=q)
        if slot[1] > 0:
            deps.append(("d", sid, slot[1]))
        self._wait(q, deps)
        ins = self.eng[q].dma_start(out=out, in_=in_, **kw)
        slot[1] += 1
        ins.then_inc(slot[0], 16)
        tag = ("d", sid, slot[1])
        for t in reads:
            t.r[("d", sid)] = tag
        for t in writes:
            if not t.r and t.w and all(x[0] == "d" and x[1] in self.qslots[q] for x in t.w):
                t.w = t.w + [tag]
            else:
                t.w = [tag]
            t.r = {}
        return ins

    def finish(self, e="sp"):
        deps = []
        for sid, slot in enumerate(self.slots):
            if slot[1] > 0:
                deps.append(("d", sid, slot[1]))
        for e2 in self.eng:
            if e2 != e and self.cnt[e2] > 0:
                deps.append(("e", e2, self.cnt[e2]))
        self._wait(e, deps)


D = 1024; L = 4; NT = 2560; NS = 2048; KC = 2816
DFF = 2816; NJ = 22
EPS = 1e-6
MLA_SCALE = 96 ** -0.5
MASKV = -30000.0
GROUPS = [(0, 512, 0), (512, 512, 0), (1024, 512, 0), (1536, 512, 0), (2048, 512, 1)]
BLOCKS = [(0, 510, 0, 1), (510, 1020, 1, 1), (1020, 1530, 1, 1), (1530, 2040, 1, 1), (2040, 2048, 1, 0),
          (2048, 2304, 0, 0), (2304, 2560, 0, 0)]
G2B = {0: [0, 1], 1: [1, 2], 2: [2, 3], 3: [3, 4], 4: [5, 6]}
B2G = {0: [0], 1: [0, 1], 2: [1, 2], 3: [2, 3], 4: [3], 5: [4], 6: [4]}
PARTS = [(0, 4), (4, 10), (10, 16), (16, 22)]
NA_BASE = [-4, 0, -2, -4, -6]


def na_variant(qt):
    return {0: 1, 1: 2, 14: 3, 15: 4}.get(qt, 0)


def na_tiles(qt):
    r0 = 2 * qt
    rs0 = min(max(r0 - 4, 0), 24)
    rs1 = min(max(r0 + 1 - 4, 0), 24)
    return list(range(rs0 // 2, (rs1 + 7) // 2 + 1))


COMP = {}


def comp(t):
    if id(t) not in COMP:
        COMP[id(t)] = (t, T(t.name + "_hi"))
    return COMP[id(t)][1]


class Arena:
    def __init__(self, nc, base, size, name):
        self.nc = nc; self.base = base; self.size = size; self.name = name
        self.off = 0; self.live = []; self.n = 0

    def reset(self, off=0):
        self.off = off

    def alloc(self, shape, dt, name="t"):
        nb = int(np.prod(shape[1:])) * (4 if dt == F32 else 2)
        nb = (nb + 63) // 64 * 64
        s = self.off; e = s + nb
        assert e <= self.size, (self.name, name, e, self.size)
        self.off = e
        self.n += 1
        h = self.nc.alloc_sbuf_tensor_at("%s_%s_%d" % (self.name, name, self.n), list(shape), dt, offset=self.base + s)
        t = T(name)
        keep = []
        for (s2, e2, t2) in self.live:
            if s2 < e and s < e2:
                tags = list(t2.r.values()) + list(t2.w)
                if id(t2) in COMP:
                    c2 = COMP[id(t2)][1]
                    tags += list(c2.r.values()) + list(c2.w)
                for tag in tags:
                    key = tag[1] if tag[0] == "e" else ("d", tag[1])
                    if key not in t.r or t.r[key][2] < tag[2]:
                        t.r[key] = tag
                if s2 < s:
                    keep.append((s2, s, t2))
                if e < e2:
                    keep.append((e, e2, t2))
            else:
                keep.append((s2, e2, t2))
        keep.append((s, e, t))
        self.live = keep
        return h.ap(), t


class _Stop(Exception):
    pass


def build_nc(depth=L, stop=None):
    nc = bass.Bass("TRN2", target_bir_lowering=False)
    es = ExitStack()
    with es:
        es.enter_context(nc.allow_low_precision("bf16 matmul operands, fp32 accumulation"))
        es.enter_context(nc.allow_non_contiguous_dma("layout"))
        S = Sched(nc, es)

        def din(name, shape):
            return nc.dram_tensor(name, list(shape), F32, kind="ExternalInput").ap()

        def dout(name, shape):
            return nc.dram_tensor(name, list(shape), F32, kind="ExternalOutput").ap()

        def dscr(name, shape):
            return nc.dram_tensor(name, list(shape), F32, kind="Internal").ap()

        xT_d = din("xT", [D, NT])
        condT_d = din("condT", [128, 8, 2])
        wmod_d = din("w_mod", [L, D, 6 * D])
        bmodT_d = din("b_modT", [L, 128, 48])
        gvec_d = din("gvec", [L, 128, 16])
        gfin_d = din("g_fin", [128, 8])
        win_d = din("w_in2", [L, D, 2112])
        wuq_d = din("w_uq2", [L, 384, 1536])
        gcq_d = din("g_cqT", [L, 128, 3])
        wukv_d = din("w_ukv2", [L, 256, 1024])
        gckv_d = din("g_ckvT", [L, 128, 2])
        gsgu_d = din("g_sgu_b", [L, 128, 256])
        wsgu_d = din("w_sguT", [L, 128, 4, 128])
        bsgu_d = din("b_sgu_b", [L, 128, 2, 128])
        wout_d = din("w_out", [L, D, D])
        wfi_d = din("w_ffn_in", [L, D, 2 * DFF])
        convp_d = din("convp", [L, 128, 4, 44])
        wfo_d = din("w_ffn_out", [L, DFF, D])
        cnak_d = din("c_na_kT", [L, 256, 256])
        cnav_d = din("c_na_v", [L, 256, 256])
        cckv_d = din("c_ckvT", [L, 256, 256])
        ckr_d = din("c_krT", [L, 32, 256])
        traw_d = din("na_traw", [L, 4, 128, 3200])
        mask_d = din("na_mask", [128, 3200])
        ropeC_d = din("ropeC", [32, NS])
        ropeS_d = din("ropeS", [32, NS])
        ident_d = din("ident", [128, 128])

        yT_d = dout("yT", [D, NT])
        okT_d = dout("o_kT", [L, 256, 512])
        ov_d = dout("o_v", [L, 512, 256])
        ockv_d = dout("o_ckvT", [L, 256, 512])
        okr_d = dout("o_krT", [L, 32, 512])

        XL = [xT_d] + [dscr("xl%d" % i, [D, NT]) for i in range(1, depth)]
        XMID = [dscr("xmid%d" % i, [D, NT]) for i in range(depth)]
        XF = [[dscr("xf%d_%d" % (i, p), [D, NT]) for p in range(3)] for i in range(depth)]
        XL_t = [[T("xl") for _ in range(7)] for _ in range(depth)]
        XMID_t = [[T("xm") for _ in range(5)] for _ in range(depth)]
        XF_t = [[[T("xf") for _ in range(7)] for p in range(3)] for _ in range(depth)]

        def xv(d):
            return d.rearrange("(k p) n -> p k n", p=128)

        BASE = 16512
        AP_ = Arena(nc, BASE, 16384, "P")
        AS_ = Arena(nc, BASE + 16384, 71680, "S")
        AM_ = Arena(nc, BASE + 16384 + 71680, 40960, "M")
        AW_ = Arena(nc, BASE + 16384 + 71680 + 40960, 212864 - 16384 - 71680 - 40960, "W")

        PS = []
        for i in range(8):
            PS.append((nc.alloc_psum_tensor("ps%d" % i, [128, 512], F32).ap(), T("ps%d" % i, excl=True)))
        rr = {"mm": 0, "acc": 0}

        def ps_mm():
            i = rr["mm"]; rr["mm"] = (i + 1) % 4
            return PS[i]

        def ps_mm6():
            i = rr.get("mm6", 0); rr["mm6"] = (i + 1) % 6
            return PS[i]

        def ps_acc():
            i = rr["acc"]; rr["acc"] = (i + 1) % 2
            return PS[4 + i]
        PSN = PS[6]
        PSMOD = PS[7]

        def chain(ps, pst, pairs, reads):
            n = len(pairs)
            for i, (l_, r_) in enumerate(pairs):
                S.op("pe", lambda e, l_=l_, r_=r_, i=i: e.matmul(ps, l_, r_, start=(i == 0), stop=(i == n - 1)),
                     reads=reads, writes=[pst], inc=(i == n - 1))

        ident, ident_t = AP_.alloc([128, 128], BF16, "ident")
        ones, ones_t = AP_.alloc([128, 128], BF16, "ones")
        ropeC, ropeC_t = AP_.alloc([128, NS], BF16, "ropeC")
        ropeS, ropeS_t = AP_.alloc([128, NS], BF16, "ropeS")
        csil, csil_t = AP_.alloc([128, 8, 2], BF16, "csil")
        condf, condf_t = AP_.alloc([128, 8, 2], F32, "condf")
        gfin, gfin_t = AP_.alloc([128, 8], F32, "gfin")
        mods = [AP_.alloc([128, 48, 2], F32, "mods%d" % i) for i in range(2)]
        A1s = [AP_.alloc([128, 8, 2], F32, "A1_%d" % i) for i in range(2)]
        A2s = [AP_.alloc([128, 8, 2], F32, "A2_%d" % i) for i in range(2)]
        bmod, bmod_t = AP_.alloc([128, 48], F32, "bmod")
        gvec, gvec_t = AP_.alloc([128, 16], F32, "gvec")
        gcq, gcq_t = AP_.alloc([128, 3], F32, "gcq")
        gckv, gckv_t = AP_.alloc([128, 2], F32, "gckv")
        convp, convp_t = AP_.alloc([128, 4, 44], F32, "convp")
        bsgu, bsgu_t = AP_.alloc([128, 2, 128], F32, "bsgu")
        gsgu, gsgu_t = AP_.alloc([128, 256], F32, "gsgu")
        wsgu, wsgu_t = AP_.alloc([128, 4, 128], BF16, "wsgu")
        small, small_t = AP_.alloc([128, 16], F32, "small")
        rsx, rsx_t = AP_.alloc([128, 512], F32, "rsx")

        S.dma("pool", ident, ident_d, writes=[ident_t])
        S.op("dve", lambda e: e.memset(ones, 1.0), writes=[ones_t])
        S.dma("pool", ropeC[64:96, :], ropeC_d, writes=[ropeC_t])
        S.dma("pool", ropeS[64:96, :], ropeS_d, writes=[ropeS_t])
        S.dma("sp", condf, condT_d, writes=[condf_t])
        S.dma("sp", gfin, gfin_d, writes=[gfin_t])
        S.op("act", lambda e: e.activation(csil, condf, AF.Silu), reads=[condf_t], writes=[csil_t])
        S.op("dve", lambda e: e.tensor_scalar(gfin, gfin, 32.0, None, ALU.mult), reads=[gfin_t], writes=[gfin_t])

        WM_SLOTS = []

        def make_wm_slots():
            WM_SLOTS.clear()
            AM_.reset(AM_.size - 4096)
            WM_SLOTS.append(AM_.alloc([128, 8, 256], BF16, "wm0"))
            AS_.reset(AS_.size - 4096)
            WM_SLOTS.append(AS_.alloc([128, 8, 256], BF16, "wm1"))

        def mod_units(l):
            par = l % 2
            md, md_t = mods[par]
            A1, A1_t = A1s[par]
            A2, A2_t = A2s[par]
            psm, psm_t = PSMOD
            units = []

            ns = len(WM_SLOTS)

            def load(j):
                wm, wm_t = WM_SLOTS[j % ns]
                S.dma("pool", wm, wmod_d[l].rearrange("(k p) n -> p k n", p=128)[:, :, 256 * j:256 * (j + 1)], writes=[wm_t])

            def blk(j):
                wm, wm_t = WM_SLOTS[j % ns]
                for oc in range(2):
                    m = 2 * j + oc
                    chain(psm[:, 2 * m:2 * m + 2], psm_t,
                          [(wm[:, k, oc * 128:(oc + 1) * 128], csil[:, k, :]) for k in range(8)], [wm_t, csil_t])
                if j + ns < 24:
                    load(j + ns)

            def fin():
                S.dma("sp", bmod, bmodT_d[l], writes=[bmod_t])
                S.dma("sp", gvec, gvec_d[l], writes=[gvec_t])
                S.op("dve", lambda e: e.tensor_tensor(md, psm[:, 0:96].rearrange("p (m c) -> p m c", c=2),
                                                       bmod.unsqueeze(2).broadcast_to([128, 48, 2]), ALU.add),
                     reads=[psm_t, bmod_t], writes=[md_t])
                S.op("dve", lambda e: e.tensor_scalar(gvec, gvec, 32.0, None, ALU.mult), reads=[gvec_t], writes=[gvec_t])
                for (A, A_t, c0, g0) in ((A1, A1_t, 8, 0), (A2, A2_t, 32, 8)):
                    S.op("dve", lambda e, A=A, c0=c0: e.tensor_scalar(A, md[:, c0:c0 + 8, :], 1.0, None, ALU.add),
                         reads=[md_t], writes=[A_t])
                    S.op("dve", lambda e, A=A, g0=g0: e.tensor_tensor(A, A, gvec[:, g0:g0 + 8].unsqueeze(2).broadcast_to([128, 8, 2]), ALU.mult),
                         reads=[A_t, gvec_t], writes=[A_t])
            units.append(lambda: [load(j_) for j_ in range(ns)])
            for j in range(24):
                units.append(lambda j=j: blk(j))
            units.append(fin)
            return units

        def norm_group(x, x_t, n, A, Bv, out, out_t, sq, sq_t, rs, rs_t, vec_reads, nk=8, eps_n=1024.0):
            psn, psn_t = PSN
            S.op("act", lambda e: e.activation(sq[:, 0:nk, 0:n], x, AF.Square), reads=[x_t], writes=[sq_t])
            chain(psn[:, 0:n], psn_t, [(ones, sq[:, k, 0:n]) for k in range(nk)], [ones_t, sq_t])
            S.op("act", lambda e: e.activation(rs[:, 0:n], psn[:, 0:n], AF.Sqrt, bias=eps_n * EPS), reads=[psn_t], writes=[rs_t])
            S.op("dve", lambda e: e.reciprocal(rs[:, 0:n], rs[:, 0:n]), reads=[rs_t], writes=[rs_t])
            S.op("pool", lambda e: e.tensor_tensor(x, x, rs[:, 0:n].unsqueeze(1).broadcast_to([128, nk, n]), ALU.mult),
                 reads=[x_t, rs_t], writes=[x_t])
            for k in range(nk):
                if Bv is None:
                    S.op("act", lambda e, k=k: e.activation(out[:, k, 0:n], x[:, k, :], AF.Identity, scale=A[:, k:k + 1]),
                         reads=[x_t] + vec_reads, writes=[out_t])
                else:
                    S.op("act", lambda e, k=k: e.activation(out[:, k, 0:n], x[:, k, :], AF.Identity, bias=Bv[:, k:k + 1], scale=A[:, k:k + 1]),
                         reads=[x_t] + vec_reads, writes=[out_t])

        def norm_stages(x, x_t, n, A, Bv, out, out_t, sq, sq_t, rs, rs_t, vec_reads, psb, mode=0):
            psn, psn_t = psb
            x_hi = comp(x_t); out_hi = comp(out_t)
            st = []
            st.append(lambda: S.op("act", lambda e: e.activation(sq[:, 0:8, 0:n], x, AF.Square), reads=[x_t, x_hi], writes=[sq_t]))
            st.append(lambda: chain(psn[:, 0:n], psn_t, [(ones, sq[:, k, 0:n]) for k in range(8)], [ones_t, sq_t]))

            def s3():
                S.op("act", lambda e: e.activation(rs[:, 0:n], psn[:, 0:n], AF.Sqrt, bias=1024.0 * EPS), reads=[psn_t], writes=[rs_t])
                S.op("dve", lambda e: e.reciprocal(rs[:, 0:n], rs[:, 0:n]), reads=[rs_t], writes=[rs_t])
                if mode == 1:
                    S.op("pool", lambda e: e.tensor_tensor(x, x, rs[:, 0:n].unsqueeze(1).broadcast_to([128, 8, n]), ALU.mult),
                         reads=[x_t, x_hi, rs_t], writes=[x_t, x_hi])
                    return
                S.op("dve", lambda e: e.tensor_tensor(x[:, 0:4, :], x[:, 0:4, :], rs[:, 0:n].unsqueeze(1).broadcast_to([128, 4, n]), ALU.mult),
                     reads=[x_t, rs_t], writes=[x_t])
                S.op("pool", lambda e: e.tensor_tensor(x[:, 4:8, :], x[:, 4:8, :], rs[:, 0:n].unsqueeze(1).broadcast_to([128, 4, n]), ALU.mult),
                     reads=[x_hi, rs_t], writes=[x_hi])
            st.append(s3)

            def s4(k0, k1):
                for k in range(k0, k1):
                    eng, xt_, ot_ = ("dve", x_t, out_t) if k < 4 else ("pool", x_hi, out_hi)
                    if mode == 1 or (mode == 2 and k < 4):
                        S.op("act", lambda e, k=k: e.activation(out[:, k, 0:n], x[:, k, :], AF.Identity, bias=Bv[:, k:k + 1], scale=A[:, k:k + 1]),
                             reads=[xt_] + vec_reads, writes=[ot_])
                        continue
                    S.op(eng, lambda e, k=k: e.tensor_scalar(out[:, k, 0:n], x[:, k, :], A[:, k:k + 1], Bv[:, k:k + 1], ALU.mult, ALU.add),
                         reads=[xt_] + vec_reads, writes=[ot_])
            st.append(lambda: (s4(0, 2), s4(4, 6)))
            st.append(lambda: (s4(2, 4), s4(6, 8)))
            return st

        def gelu(dst, dst_t, src, src_t, t1, t1_t, t2, t2_t, eng2="pool"):
            S.op("act", lambda e: e.activation(dst, src, AF.Gelu_apprx_tanh), reads=[src_t], writes=[dst_t])
            return
            S.op("act", lambda e: e.activation(t1, src, AF.Square), reads=[src_t], writes=[t1_t])
            S.op("dve", lambda e: e.tensor_scalar(t1, t1, 0.044715, 1.0, ALU.mult, ALU.add), reads=[t1_t], writes=[t1_t])
            S.op("dve", lambda e: e.tensor_tensor(t1, t1, src, ALU.mult), reads=[t1_t, src_t], writes=[t1_t])
            S.op("act", lambda e: e.activation(t2, t1, AF.Sigmoid, scale=1.5957691216057308), reads=[t1_t], writes=[t2_t])
            S.op("dve", lambda e: e.tensor_tensor(dst, t2, src, ALU.mult), reads=[t2_t, src_t], writes=[dst_t])

        DBG = {}

        def dbg(name, ap, t):
            shp = list(ap.shape)
            d_ = nc.dram_tensor("dbg_" + name, shp, ap.dtype, kind="ExternalOutput").ap()
            S.dma("sp", d_, ap, reads=[t])
            DBG[name] = d_

        LST = [None]

        WIN_NEXT = [None]

        def load_win(l, win, win_t):
            wv = win_d[l].rearrange("(k p) n -> p k n", p=128)
            for c0 in range(0, 2112, 528):
                S.dma("pool", win[:, :, c0:c0 + 528], wv[:, :, c0:c0 + 528], writes=[win_t])

        def cp(tag):
            if LST[0] == "p1:" + tag:
                raise _Stop()

        def layer(l):
            LST[0] = (stop[3:] if (stop and stop.startswith("L%d:" % l)) else (stop if (l == 0 and stop and not stop.startswith("L")) else None))
            par = l % 2
            md, md_t = mods[par]
            A1, A1_t = A1s[par]
            A2, A2_t = A2s[par]
            last = (l == depth - 1)

            S.dma("sp", gcq, gcq_d[l], writes=[gcq_t])
            S.dma("sp", gckv, gckv_d[l], writes=[gckv_t])
            S.dma("sp", convp, convp_d[l], writes=[convp_t])
            S.dma("sp", bsgu, bsgu_d[l], writes=[bsgu_t])
            S.dma("sp", gsgu, gsgu_d[l], writes=[gsgu_t])
            S.dma("pool", wsgu, wsgu_d[l], writes=[wsgu_t])
            S.op("dve", lambda e: e.tensor_scalar(gcq, gcq, float(np.sqrt(384.0)), None, ALU.mult), reads=[gcq_t], writes=[gcq_t])
            S.op("dve", lambda e: e.tensor_scalar(gckv, gckv, 16.0, None, ALU.mult), reads=[gckv_t], writes=[gckv_t])
            S.op("dve", lambda e: e.tensor_scalar(gsgu, gsgu, 16.0, None, ALU.mult), reads=[gsgu_t], writes=[gsgu_t])

            AS_.reset()
            qaT, qaT_t = AS_.alloc([128, 2, NT], BF16, "qaT")
            kaT, kaT_t = AS_.alloc([128, 2, KC], BF16, "kaT")
            Vna, Vna_t = AS_.alloc([128, 22, 384], BF16, "Vna")
            cqnT, cqnT_t = AS_.alloc([128, 3, NT], BF16, "cqnT")
            ckvnT, ckvnT_t = AS_.alloc([128, 2, KC], BF16, "ckvnT")
            krT, krT_t = AS_.alloc([128, KC], BF16, "krT")
            mixT = nc.alloc_sbuf_tensor_at("mixT_%d" % l, [128, 8, NT], BF16, offset=AM_.base).ap()
            AM_.reset()
            sq, sq_t = AM_.alloc([128, 8, 512], BF16, "sq")
            cqf, cqf_t = AM_.alloc([128, 3, 512], F32, "cqf")
            sqc, sqc_t = AM_.alloc([128, 3, 512], BF16, "sqc")
            g1, g1_t = AM_.alloc([128, 512], F32, "g1")
            g2, g2_t = AM_.alloc([128, 512], F32, "g2")
            uT, uT_t = AM_.alloc([128, 2, 512], F32, "uT")
            rs, rs_t = AM_.alloc([128, 512], F32, "rs")
            assert AM_.off <= 6 * NT * 2
            AM_.reset(6 * NT * 2)
            _, mixC_t = AM_.alloc([128, 2, NT], BF16, "mixC")

            if WIN_NEXT[0] is not None:
                win, win_t = WIN_NEXT[0]
                WIN_NEXT[0] = None
            else:
                AW_.reset(45056)
                win, win_t = AW_.alloc([128, 8, 2112], BF16, "win")
                load_win(l, win, win_t)
            AW_.reset(45056 + 33792)
            vnpad, vnpad_t = AW_.alloc([128, 4, 512], BF16, "vnpad")
            AW_.reset()
            stg = [AW_.alloc([128, 512], F32, "stg%d" % i) for i in range(3)]
            xg, xg_t = AW_.alloc([128, 8, 512], F32, "xg")
            hgs = [AW_.alloc([128, 8, 512], BF16, "hg%d" % i) for i in range(2)]
            ty4, ty4_t = AW_.alloc([128, 4, 256], F32, "ty4")
            tq, tq_t = AW_.alloc([128, 256], F32, "tq")
            assert AW_.off <= 45056
            stg_i = [0]

            def stage():
                i = stg_i[0]; stg_i[0] = (i + 1) % len(stg)
                return stg[i]

            for c in range(2):
                S.dma("pool", kaT[:, c, 0:256], cnak_d[l][c * 128:(c + 1) * 128, :], writes=[kaT_t])
                S.dma("pool", ckvnT[:, c, 0:256], cckv_d[l][c * 128:(c + 1) * 128, :], writes=[ckvnT_t])
            S.dma("pool", krT[64:96, 0:256], ckr_d[l], writes=[krT_t])
            S.op("pool", lambda e: e.memset(Vna.rearrange("p t (a b) -> p t a b", b=192)[:, :, :, 64:128], 1.0), writes=[Vna_t])
            for t_ in range(2):
                src = cnav_d[l][t_ * 128:(t_ + 1) * 128, :].rearrange("p (a h d) -> p a h d", a=2, h=2)
                dstv = Vna[:, t_, :].rearrange("p (a b) -> p a b", b=192)
                S.dma("pool", dstv[:, :, 0:64], src[:, :, 0, :], writes=[Vna_t])
                S.dma("pool", dstv[:, :, 128:192], src[:, :, 1, :], writes=[Vna_t])
            S.op("pool", lambda e: e.memset(vnpad, 0.0), writes=[vnpad_t])

            cp("ld")
            xsrc = xv(XL[l])

            def p1_norm(gi):
                c0_, n_, ci_ = GROUPS[gi]
                xreads = [] if l == 0 else [XL_t[l][b] for b in G2B[gi]]
                S.dma("sp", xg, xsrc[:, :, c0_:c0_ + n_], reads=xreads, writes=[xg_t, comp(xg_t)])
                hg_, hg_t_ = hgs[gi % 2]
                return norm_stages(xg, xg_t, n_, A1[:, :, ci_], md[:, 0:8, ci_], hg_, hg_t_, sq, sq_t, rsx, rsx_t, [A1_t, md_t], PSMOD)
            for f_ in p1_norm(0):
                f_()
            sgu_pend = []
            for gi, (c0, n, ci) in enumerate(GROUPS):
                cols = slice(c0, c0 + n)
                kcols = slice(256 + c0, 256 + c0 + n)
                hg, hg_t = hgs[gi % 2]
                nxt = p1_norm(gi + 1) if gi + 1 < len(GROUPS) else []

                def nstage():
                    if nxt:
                        nxt.pop(0)()
                rd = [win_t, hg_t, comp(hg_t)]

                def fm(col0, M):
                    ps, pst = ps_mm6()
                    chain(ps[0:M, 0:n], pst, [(win[:, k, col0:col0 + M], hg[:, k, :]) for k in range(8)], rd)
                    return ps, pst
                for c in range(2):
                    ps, pst = fm(c * 128, 128)
                    S.op("act", lambda e, ps=ps, c=c: e.activation(qaT[:, c, cols], ps[:, 0:n], AF.Copy, scale=0.125),
                         reads=[pst], writes=[qaT_t])
                for c in range(2):
                    ps, pst = fm(256 + c * 128, 128)
                    S.op("dve", lambda e, ps=ps, c=c: e.tensor_copy(kaT[:, c, kcols], ps[:, 0:n]), reads=[pst], writes=[kaT_t])
                    if ci == 1:
                        st, st_t = stage()
                        S.op("act", lambda e, ps=ps, st=st: e.copy(st[:, 0:n], ps[:, 0:n]), reads=[pst], writes=[st_t])
                        S.dma("sp", okT_d[l][c * 128:(c + 1) * 128, :], st[:, 0:n], reads=[st_t])
                nstage()
                def nrm_proj(colb, nch, gv, gv_t):
                    for c in range(nch):
                        ps, pst = fm(colb + c * 128, 128)
                        S.op("act", lambda e, ps=ps, c=c, gv=gv: e.activation(cqf[:, c, 0:n], ps[:, 0:n], AF.Copy, scale=gv[:, c:c + 1]),
                             reads=[pst, gv_t], writes=[cqf_t])
                        S.op("act", lambda e, ps=ps, c=c: e.activation(sqc[:, c, 0:n], ps[:, 0:n], AF.Square), reads=[pst], writes=[sqc_t])

                def nrm_fin(nch, epsn, dstT, dst_t, dcols, is_ckv):
                    psn, psn_t = PSN
                    chain(psn[:, 0:n], psn_t, [(ones, sqc[:, c, 0:n]) for c in range(nch)], [ones_t, sqc_t])
                    S.op("act", lambda e, epsn=epsn: e.activation(rs[:, 0:n], psn[:, 0:n], AF.Sqrt, bias=epsn * EPS), reads=[psn_t], writes=[rs_t])
                    S.op("dve", lambda e: e.reciprocal(rs[:, 0:n], rs[:, 0:n]), reads=[rs_t], writes=[rs_t])
                    S.op("dve", lambda e, nch=nch, dstT=dstT, dcols=dcols: e.tensor_tensor(
                        dstT[:, :, dcols], cqf[:, 0:nch, 0:n], rs[:, 0:n].unsqueeze(1).broadcast_to([128, nch, n]), ALU.mult),
                        reads=[cqf_t, rs_t], writes=[dst_t])
                    if is_ckv and ci == 1:
                        for c in range(2):
                            st, st_t = stage()
                            S.op("pool", lambda e, st=st, c=c: e.tensor_tensor(st[:, 0:n], cqf[:, c, 0:n], rs[:, 0:n], ALU.mult),
                                 reads=[cqf_t, rs_t], writes=[st_t])
                            S.dma("sp", ockv_d[l][c * 128:(c + 1) * 128, :], st[:, 0:n], reads=[st_t])

                def krope():
                    psA, psA_t = fm(1152, 96)
                    if ci == 0:
                        psB, psB_t = fm(1248, 96)
                        S.op("dve", lambda e, psA=psA: e.tensor_tensor(g1[64:96, 0:n], psA[64:96, 0:n], ropeC[64:96, cols], ALU.mult),
                             reads=[psA_t, ropeC_t], writes=[g1_t])
                        S.op("dve", lambda e, psB=psB: e.tensor_tensor(g2[64:96, 0:n], psB[64:96, 0:n], ropeS[64:96, cols], ALU.mult),
                             reads=[psB_t, ropeS_t], writes=[g2_t])
                        S.op("pool", lambda e: e.tensor_tensor(krT[64:96, kcols], g1[64:96, 0:n], g2[64:96, 0:n], ALU.add),
                             reads=[g1_t, g2_t], writes=[krT_t])
                    else:
                        S.op("dve", lambda e, psA=psA: e.tensor_copy(krT[64:96, kcols], psA[64:96, 0:n]), reads=[psA_t], writes=[krT_t])
                        st, st_t = stage()
                        S.op("act", lambda e, psA=psA, st=st: e.copy(st[64:96, 0:n], psA[64:96, 0:n]), reads=[psA_t], writes=[st_t])
                        S.dma("sp", okr_d[l], st[64:96, 0:n], reads=[st_t])

                def uproj():
                    for c in range(2):
                        ps, pst = fm(1344 + c * 128, 128)
                        gelu(uT[:, c, 0:n], uT_t, ps[:, 0:n], pst, g1[:, 0:n], g1_t, g2[:, 0:n], g2_t)

                if sgu_pend:
                    sgu_pend.pop(0)()
                nrm_proj(512, 3, gcq, gcq_t)
                nstage()
                krope()
                nrm_fin(3, 384.0, cqnT, cqnT_t, cols, False)
                nstage()
                nrm_proj(896, 2, gckv, gckv_t)
                nstage()
                uproj()
                nrm_fin(2, 256.0, ckvnT, ckvnT_t, kcols, True)
                nstage()
                for tt in range(4):
                    ps, pst = ps_mm6()
                    chain(ps, pst, [(hg[:, k, tt * 128:(tt + 1) * 128], win[:, k, 1600:2112]) for k in range(8)], rd)
                    kt = 2 + (c0 // 128) + tt
                    dstv = Vna[:, kt, :].rearrange("p (a b) -> p a b", b=192)
                    srcv = ps[:, 0:256].rearrange("p (a h d) -> p a h d", a=2, h=2)
                    S.op("act", lambda e, dstv=dstv, srcv=srcv: e.copy(dstv[:, :, 0:64], srcv[:, :, 0, :]), reads=[pst], writes=[Vna_t])
                    S.op("act", lambda e, dstv=dstv, srcv=srcv: e.copy(dstv[:, :, 128:192], srcv[:, :, 1, :]), reads=[pst], writes=[Vna_t])
                    if ci == 1:
                        st, st_t = stage()
                        S.op("dve", lambda e, ps=ps, st=st: e.tensor_copy(st[:, 0:256], ps[:, 0:256]), reads=[pst], writes=[st_t])
                        S.dma("sp", ov_d[l][tt * 128:(tt + 1) * 128, :], st[:, 0:256], reads=[st_t])
                    S.op("act", lambda e, ps=ps, tt=tt: e.activation(ty4[:, tt, :], ps[:, 256:512], AF.Gelu_apprx_tanh), reads=[pst], writes=[ty4_t])
                    S.op("act", lambda e, tt=tt: e.activation(tq, ty4[:, tt, :], AF.Square, accum_out=small[:, tt:tt + 1]), reads=[ty4_t], writes=[tq_t, small_t])
                S.op("act", lambda e: e.activation(small[:, 4:8], small[:, 0:4], AF.Sqrt, bias=256.0 * EPS), reads=[small_t], writes=[small_t])
                S.op("dve", lambda e: e.reciprocal(small[:, 4:8], small[:, 4:8]), reads=[small_t], writes=[small_t])
                for tt in range(4):
                    vp = vnpad[:, tt, :].rearrange("p (a b) -> p a b", b=256)
                    yv = ty4[:, tt, :].rearrange("p (a b) -> p a b", b=128)
                    gv2 = gsgu.rearrange("p (a b) -> p a b", b=128)
                    S.op("dve", lambda e, vp=vp, yv=yv, gv2=gv2, tt=tt: e.scalar_tensor_tensor(vp[:, :, 0:64], yv[:, :, 0:64], small[:, 4 + tt:5 + tt], gv2[:, :, 0:64], ALU.mult, ALU.mult),
                         reads=[ty4_t, small_t, gsgu_t], writes=[vnpad_t])
                    S.op("dve", lambda e, vp=vp, yv=yv, gv2=gv2, tt=tt: e.scalar_tensor_tensor(vp[:, :, 192:256], yv[:, :, 64:128], small[:, 4 + tt:5 + tt], gv2[:, :, 64:128], ALU.mult, ALU.mult),
                         reads=[ty4_t, small_t, gsgu_t], writes=[vnpad_t])
                while nxt:
                    nstage()
                def sgu(cols=cols):
                    for pr in range(2):
                        ps, pst = ps_mm6()
                        for tt in range(4):
                            for q in range(2):
                                gq = 2 * pr + q
                                S.op("pe", lambda e, ps=ps, tt=tt, gq=gq, q=q: e.matmul(
                                    ps[:, tt * 128:(tt + 1) * 128], vnpad[:, tt, gq * 128:(gq + 1) * 128], wsgu[:, gq, :],
                                    start=(q == 0), stop=(q == 1)), reads=[vnpad_t, wsgu_t], writes=[pst])
                        S.op("dve", lambda e, ps=ps, pr=pr: e.tensor_tensor(
                            g1.rearrange("p (a b) -> p a b", b=128), ps.rearrange("p (a b) -> p a b", b=128),
                            bsgu[:, pr, :].unsqueeze(1).broadcast_to([128, 4, 128]), ALU.add), reads=[pst, bsgu_t], writes=[g1_t])
                        S.op("pool", lambda e, pr=pr: e.tensor_tensor(mixT[:, 6 + pr, cols], g1, uT[:, pr, :], ALU.mult),
                             reads=[g1_t, uT_t], writes=[mixC_t])
                sgu_pend.append(sgu)

            AW_.reset()
            tabs = [AW_.alloc([128, 5, 640], BF16, "tab%d" % i) for i in range(1)]
            maskb, maskb_t = AW_.alloc([128, 3200], BF16, "maskb")
            S.dma("pool", maskb, mask_d, writes=[maskb_t])

            tab_ts = [T("tabv%d" % v) for v in range(5)]
            for v_ in range(5):
                for tag_ in list(tabs[0][1].r.values()) + list(tabs[0][1].w):
                    key_ = tag_[1] if tag_[0] == "e" else ("d", tag_[1])
                    tab_ts[v_].r[key_] = tag_

            def load_tab(h, vs=(1, 2, 0, 3, 4)):
                tab = tabs[0][0]
                for v_ in vs:
                    S.dma("pool", tab[:, v_, :], traw_d[l, h][:, v_ * 640:(v_ + 1) * 640], writes=[tab_ts[v_]])
                    S.op("pool", lambda e, v_=v_: e.tensor_tensor(tab[:, v_, :], tab[:, v_, :], maskb[:, v_ * 640:(v_ + 1) * 640], ALU.add),
                         reads=[tab_ts[v_], maskb_t], writes=[tab_ts[v_]])
            load_tab(0)
            while sgu_pend:
                sgu_pend.pop(0)()
            cp("end")
            if LST[0] == "p1":
                dbg("qaT", qaT, qaT_t); dbg("kaT", kaT, kaT_t); dbg("Vna", Vna, Vna_t); dbg("cqnT", cqnT, cqnT_t)
                dbg("ckvnT", ckvnT, ckvnT_t); dbg("krT", krT, krT_t); dbg("mixC", mixT[:, 6:8, :], mixC_t)
                raise _Stop()
            AM_.reset()
            _, mixA_t = AM_.alloc([128, 6, NT], BF16, "mixAB")
            mixB_t = mixA_t
            AW_.reset(12800)
            PTs = [AW_.alloc([128, 512], BF16, "PT%d" % i) for i in range(4)]
            rcs = [AW_.alloc([128, 512], F32, "rc%d" % i) for i in range(1)]
            wuq, wuq_t = AW_.alloc([128, 3, 1536], BF16, "wuq")
            wukv, wukv_t = AW_.alloc([128, 2, 1024], BF16, "wukv")
            Vmla, Vmla_t = AW_.alloc([128, 22, 768], BF16, "Vmla")
            Khs = [AW_.alloc([128, KC], BF16, "Kh%d" % i) for i in range(2)]
            Qhs = [AW_.alloc([128, 512], BF16, "Qh%d" % i) for i in range(2)]
            qt1, qt1_t = AW_.alloc([128, 512], F32, "qt1")
            qt2, qt2_t = AW_.alloc([128, 512], F32, "qt2")
            pt_i = [0]; rc_i = [0]

            def nextPT():
                i = pt_i[0]; pt_i[0] = (i + 1) % 4
                return PTs[i]

            def nextrc():
                i = rc_i[0]; rc_i[0] = (i + 1) % 1
                return rcs[i]

            def normalize(psO, psO_t, po, n, dst, dst_t):
                rc, rc_t = nextrc()
                dr = slice(64 - po, 128 - po)
                orr = slice(po, po + 64)
                S.op("dve", lambda e: e.reciprocal(rc[dr, 0:n], psO[dr, 0:n]), reads=[psO_t], writes=[rc_t])
                S.op("dve", lambda e: e.tensor_tensor(dst, psO[orr, 0:n], rc[dr, 0:n], ALU.mult), reads=[psO_t, rc_t], writes=[dst_t])

            SEQS = [(qg * 512, 512, list(range(18)), True) for qg in range(4)] + \
                   [(NS, 256, [18, 19], False), (NS + 256, 256, [20, 21], False)]
            def build_K(h):
                Kh, Kh_t = Khs[h % 2]
                for kg in range(0, KC, 512):
                    n = min(512, KC - kg)
                    ps, pst = ps_mm()
                    chain(ps[0:64, 0:n], pst, [(wukv[:, kc, 64 * h:64 * h + 64], ckvnT[:, kc, kg:kg + n]) for kc in range(2)], [wukv_t, ckvnT_t])
                    S.op("dve", lambda e, ps=ps, n=n, kg=kg: e.tensor_copy(Kh[0:64, kg:kg + n], ps[0:64, 0:n]), reads=[pst], writes=[Kh_t])
                S.op("pool", lambda e: e.tensor_copy(Kh[64:96, :], krT[64:96, :]), reads=[krT_t], writes=[Kh_t])

            def build_Q(h, si):
                q0, n, kts, rope = SEQS[si]
                qcols = slice(q0, q0 + n)
                Qh, Qh_t = Qhs[(h * 6 + si) % len(Qhs)]
                psA, psA_t = PSN
                chain(psA[0:96, 0:n], psA_t, [(wuq[:, kc, h * 192:h * 192 + 96], cqnT[:, kc, qcols]) for kc in range(3)], [wuq_t, cqnT_t])
                S.op("dve", lambda e: e.tensor_copy(Qh[0:64, 0:n], psA[0:64, 0:n]), reads=[psA_t], writes=[Qh_t])
                if rope:
                    psB, psB_t = PSMOD
                    chain(psB[0:96, 0:n], psB_t, [(wuq[:, kc, h * 192 + 96:h * 192 + 192], cqnT[:, kc, qcols]) for kc in range(3)], [wuq_t, cqnT_t])
                    S.op("dve", lambda e: e.tensor_tensor(qt1[64:96, 0:n], psA[64:96, 0:n], ropeC[64:96, qcols], ALU.mult),
                         reads=[psA_t, ropeC_t], writes=[qt1_t])
                    S.op("dve", lambda e: e.tensor_tensor(qt2[64:96, 0:n], psB[64:96, 0:n], ropeS[64:96, qcols], ALU.mult),
                         reads=[psB_t, ropeS_t], writes=[qt2_t])
                    S.op("pool", lambda e: e.tensor_tensor(Qh[64:96, 0:n], qt1[64:96, 0:n], qt2[64:96, 0:n], ALU.add),
                         reads=[qt1_t, qt2_t], writes=[Qh_t])
                else:
                    S.op("dve", lambda e: e.tensor_copy(Qh[64:96, 0:n], psA[64:96, 0:n]), reads=[psA_t], writes=[Qh_t])

            ORDER = [(h_, si_) for h_ in range(8) for si_ in range(len(SEQS))]
            nbuilt = [0]
            pro_done = [False]

            def build_next(cur):
                if nbuilt[0] < len(ORDER) and nbuilt[0] <= cur + 5:
                    build_Q(*ORDER[nbuilt[0]])
                    nbuilt[0] += 1

            def mla_prologue():
                pro_done[0] = True
                AW_.reset(6400)
                for i_ in range(4):
                    Qhs.append(AW_.alloc([128, 512], BF16, "Qhx%d" % i_))
                build_K(0)
                build_next(0)
                build_next(0)
            S.dma("pool", wukv, wukv_d[l].rearrange("(k p) n -> p k n", p=128), writes=[wukv_t])
            S.dma("pool", wuq, wuq_d[l].rearrange("(k p) n -> p k n", p=128), writes=[wuq_t])
            S.op("pool", lambda e: e.memset(Vmla.rearrange("p t (a b) -> p t a b", b=192)[:, :, :, 64:128], 1.0), writes=[Vmla_t])
            vb_next = [0]

            def vbuild(cnt):
                for _ in range(cnt):
                    kt = vb_next[0]
                    if kt >= 22:
                        return
                    vb_next[0] += 1
                    ps, pst = ps_mm()
                    chain(ps, pst, [(ckvnT[:, kc, kt * 128:(kt + 1) * 128], wukv[:, kc, 512:1024]) for kc in range(2)], [ckvnT_t, wukv_t])
                    dstv = Vmla[:, kt, :].rearrange("p (a b) -> p a b", b=192)
                    srcv = ps.rearrange("p (a h d) -> p a h d", a=4, h=2)
                    S.op("dve", lambda e, dstv=dstv, srcv=srcv: e.tensor_copy(dstv[:, :, 0:64], srcv[:, :, 0, :]), reads=[pst], writes=[Vmla_t])
                    S.op("dve", lambda e, dstv=dstv, srcv=srcv: e.tensor_copy(dstv[:, :, 128:192], srcv[:, :, 1, :]), reads=[pst], writes=[Vmla_t])
            pend = []
            for h in range(4):
                pr = h // 2; po = (h % 2) * 64
                prs = slice(po, po + 64)
                tab, tab_t = tabs[0]
                if h > 0:
                    load_tab(h, (0, 3, 4))
                    vbuild(5)
                for qg in range(4):
                    psO, psO_t = ps_acc()
                    for qq in range(4):
                        qt = qg * 4 + qq
                        qcols = slice(qt * 128, (qt + 1) * 128)
                        v = na_variant(qt)
                        blocks = [("c", 0), ("c", 1)] + [("l", kt) for kt in na_tiles(qt)]
                        banks = []
                        for b0 in range(0, len(blocks), 4):
                            sub = blocks[b0:b0 + 4]
                            ps, pst = ps_mm()
                            for bi, (kind, kt) in enumerate(sub):
                                dst = ps[:, bi * 128:(bi + 1) * 128]
                                if kind == "c":
                                    S.op("pe", lambda e, dst=dst, kt=kt: e.matmul(dst, kaT[prs, pr, kt * 128:(kt + 1) * 128], qaT[prs, pr, qcols], start=True, stop=True),
                                         reads=[kaT_t, qaT_t], writes=[pst])
                                else:
                                    j = (2 * kt - 2 * qt) - NA_BASE[v]
                                    S.op("pe", lambda e, dst=dst, kt=kt: e.matmul(dst, kaT[prs, pr, 256 + kt * 128:256 + (kt + 1) * 128], qaT[prs, pr, qcols], start=True, stop=False),
                                         reads=[kaT_t, qaT_t], writes=[pst], inc=False)
                                    S.op("pe", lambda e, dst=dst, j=j: e.matmul(dst, tab[:, v, j * 64:j * 64 + 128], ident, start=False, stop=True),
                                         reads=[tab_ts[v], ident_t], writes=[pst])
                            PT, PT_t = nextPT()
                            w = len(sub) * 128
                            S.op("act", lambda e, ps=ps, PT=PT, w=w: e.activation(PT[:, 0:w], ps[:, 0:w], AF.Exp), reads=[pst], writes=[PT_t])
                            banks.append((PT, PT_t, sub))
                        def pv(banks=banks, psO=psO, psO_t=psO_t, qq=qq, nb=len(blocks), pr=pr, po=po):
                            cnt = 0
                            for (PT, PT_t, sub) in banks:
                                for bi, (kind, kt) in enumerate(sub):
                                    vt = kt if kind == "c" else 2 + kt
                                    S.op("pe", lambda e, PT=PT, bi=bi, vt=vt, cnt=cnt: e.matmul(
                                        psO[:, qq * 128:(qq + 1) * 128], Vna[:, vt, pr * 192 + po:pr * 192 + po + 128], PT[:, bi * 128:(bi + 1) * 128],
                                        start=(cnt == 0), stop=(cnt == nb - 1)), reads=[Vna_t, PT_t], writes=[psO_t])
                                    cnt += 1
                        for f_ in pend:
                            f_()
                        pend = [pv]
                        if qq == 3:
                            pend.append(lambda psO=psO, psO_t=psO_t, po=po, prs=prs, pr=pr, qg=qg:
                                        normalize(psO, psO_t, po, 512, mixT[prs, pr, qg * 512:(qg + 1) * 512], mixA_t))
                            if h >= 1:
                                vbuild(1)
                            if h == 3 and qg == 2:
                                mla_prologue()
                            if qg == 0 and h + 1 < 4:
                                load_tab(h + 1, (1, 2))
            for f_ in pend:
                f_()
            for sq_i in range(2):
                qcols = slice(NS + sq_i * 256, NS + (sq_i + 1) * 256)
                for h in range(4):
                    pr = h // 2; po = (h % 2) * 64
                    prs = slice(po, po + 64)
                    ps, pst = ps_mm()
                    for b in range(2):
                        kc0 = 256 + NS + sq_i * 256 + b * 128
                        S.op("pe", lambda e, ps=ps, b=b, kc0=kc0: e.matmul(ps[:, b * 256:(b + 1) * 256], kaT[prs, pr, kc0:kc0 + 128], qaT[prs, pr, qcols], start=True, stop=True),
                             reads=[kaT_t, qaT_t], writes=[pst])
                    PT, PT_t = nextPT()
                    S.op("act", lambda e, ps=ps, PT=PT: e.activation(PT, ps, AF.Exp), reads=[pst], writes=[PT_t])
                    psO, psO_t = ps_acc()
                    for b in range(2):
                        vt = 18 + sq_i * 2 + b
                        S.op("pe", lambda e, PT=PT, b=b, vt=vt: e.matmul(psO[:, 0:256], Vna[:, vt, pr * 192 + po:pr * 192 + po + 128], PT[:, b * 256:(b + 1) * 256],
                                                                         start=(b == 0), stop=(b == 1)), reads=[Vna_t, PT_t], writes=[psO_t])
                    normalize(psO, psO_t, po, 256, mixT[prs, pr, qcols], mixA_t)

            for v_ in range(5):
                for tag_ in list(tab_ts[v_].r.values()) + list(tab_ts[v_].w):
                    key_ = tag_[1] if tag_[0] == "e" else ("d", tag_[1])
                    if key_ not in tabs[0][1].r or tabs[0][1].r[key_][2] < tag_[2]:
                        tabs[0][1].r[key_] = tag_
            AS_.reset(0)
            wout, wout_t = AS_.alloc([128, 8, D], BF16, "wout")
            wout_ts = [T("wout%d" % o) for o in range(8)]
            for o in range(8):
                for tag_ in list(wout_t.r.values()) + list(wout_t.w):
                    key_ = tag_[1] if tag_[0] == "e" else ("d", tag_[1])
                    wout_ts[o].r[key_] = tag_
                S.dma("pool", wout[:, :, o * 128:(o + 1) * 128], wout_d[l].rearrange("(k p) n -> p k n", p=128)[:, :, o * 128:(o + 1) * 128], writes=[wout_ts[o]])
            if LST[0] == "na":
                dbg("mixA", mixT[:, 0:2, :], mixA_t)
                raise _Stop()
            vbuild(22)
            LA = 3
            mpend = []
            if not pro_done[0]:
                mla_prologue()
            for h in range(8):
                pr = h // 2; po = (h % 2) * 64
                prs = slice(po, po + 64)
                Kh, Kh_t = Khs[h % 2]
                for si, (q0, n, kts, rope) in enumerate(SEQS):
                    qcols = slice(q0, q0 + n)
                    Qh, Qh_t = Qhs[(h * 6 + si) % len(Qhs)]
                    cur = h * 6 + si
                    while nbuilt[0] <= cur:
                        build_next(cur)
                    psO, psO_t = ps_acc()
                    nk = len(kts)
                    for i, kt in enumerate(kts):
                        ps, pst = ps_mm()
                        S.op("pe", lambda e, ps=ps, kt=kt: e.matmul(ps[:, 0:n], Kh[0:96, kt * 128:(kt + 1) * 128], Qh[0:96, 0:n], start=True, stop=True),
                             reads=[Kh_t, Qh_t], writes=[pst])
                        PT, PT_t = nextPT()
                        S.op("act", lambda e, ps=ps, PT=PT: e.activation(PT[:, 0:n], ps[:, 0:n], AF.Exp, scale=MLA_SCALE), reads=[pst], writes=[PT_t])

                        def pv(PT=PT, PT_t=PT_t, kt=kt, i=i, nk=nk, psO=psO, psO_t=psO_t, n=n, qcols=qcols, pr=pr, po=po, prs=prs):
                            S.op("pe", lambda e: e.matmul(psO[:, 0:n], Vmla[:, kt, pr * 192 + po:pr * 192 + po + 128], PT[:, 0:n],
                                                          start=(i == 0), stop=(i == nk - 1)), reads=[Vmla_t, PT_t], writes=[psO_t])
                            if i == nk - 1:
                                normalize(psO, psO_t, po, n, mixT[prs, 2 + pr, qcols], mixB_t)
                        mpend.append(pv)
                        if len(mpend) > LA:
                            mpend.pop(0)()
                        if nk > 4 and i in (3, 7, 11):
                            build_next(cur)
                        if si == 1 and i == 14 and h + 1 < 8:
                            build_K(h + 1)
            for f_ in mpend:
                f_()

            if LST[0] == "mla":
                dbg("mixA", mixT[:, 0:6, :], mixA_t); dbg("Vmla", Vmla, Vmla_t); dbg("Kh", Khs[0][0], Khs[0][1])
                raise _Stop()
            AS_.reset(16384)
            h2T, h2T_t = AS_.alloc([128, 8, NT], BF16, "h2T")
            gTs = [AS_.alloc([128, 6, 512], BF16, "gT%d" % i) for i in range(1)]
            sq2, sq2_t = AS_.alloc([128, 8, 512], BF16, "sq2")
            AW_.reset()
            xgs = [AW_.alloc([128, 8, 512], F32, "xg%d" % i) for i in range(2)]
            rs2, rs2_t = AW_.alloc([128, 512], F32, "rs2")
            cts = [AW_.alloc([128, 512], F32, "ct%d" % i) for i in range(4)]
            sgs = [AW_.alloc([128, 512], BF16, "sg%d" % i) for i in range(2)]
            wslotA_fi = AW_.alloc([128, 8, 2, 768], BF16, "wfiA")
            wslotA_fo = AW_.alloc([128, 6, D], BF16, "wfoA")

            def load_part(p, slot_fi, slot_fo):
                j0, j1 = PARTS[p]
                nj = j1 - j0
                fi, fi_t = slot_fi
                fo, fo_t = slot_fo
                wv_ = wfi_d[l].rearrange("(k p) n -> p k n", p=128)
                S.dma("pool", fi[:, :, 0, 0:nj * 128], wv_[:, :, 128 * j0:128 * j1], writes=[fi_t])
                S.dma("pool", fi[:, :, 1, 0:nj * 128], wv_[:, :, DFF + 128 * j0:DFF + 128 * j1], writes=[fi_t])
                S.dma("pool", fo[:, 0:nj, :], wfo_d[l][128 * j0:128 * j1, :].rearrange("(k p) n -> p k n", p=128), writes=[fo_t])
            load_part(0, wslotA_fi, wslotA_fo)

            mix_reads = [mixA_t, mixC_t]
            p3prev = []
            for gi, (c0, n, ci) in enumerate(GROUPS):
                cols = slice(c0, c0 + n)
                xg, xg_t = xgs[gi % 2]
                xreads = [] if l == 0 else [XL_t[l][b] for b in G2B[gi]]
                S.dma("sp", xg, xsrc[:, :, cols], reads=xreads, writes=[xg_t, comp(xg_t)])
                for o in range(8):
                    ps, pst = ps_mm6()
                    chain(ps[:, 0:n], pst, [(wout[:, k, o * 128:(o + 1) * 128], mixT[:, k, cols]) for k in range(8)], [wout_ts[o]] + mix_reads)
                    S.op("dve", lambda e, ps=ps, o=o, xg=xg: e.scalar_tensor_tensor(xg[:, o, :], ps[:, 0:n], md[:, 16 + o, ci:ci + 1], xg[:, o, :], ALU.mult, ALU.add),
                         reads=[pst, md_t, (xg_t if o < 4 else comp(xg_t))], writes=[(xg_t if o < 4 else comp(xg_t))])
                    if p3prev:
                        p3prev.pop(0)()
                while p3prev:
                    p3prev.pop(0)()
                S.dma("sp", xv(XMID[l])[:, :, cols], xg, reads=[xg_t, comp(xg_t)], writes=[XMID_t[l][gi]])
                h2v = h2T[:, :, cols]
                p3prev = norm_stages(xg, xg_t, n, A2[:, :, ci], md[:, 24:32, ci], h2v, h2T_t, sq2, sq2_t, rs2, rs2_t, [A2_t, md_t], PSN, mode=2)
            while p3prev:
                p3prev.pop(0)()
            for o in range(8):
                for tag_ in list(wout_ts[o].r.values()) + list(wout_ts[o].w):
                    key_ = tag_[1] if tag_[0] == "e" else ("d", tag_[1])
                    if key_ not in wout_t.r or wout_t.r[key_][2] < tag_[2]:
                        wout_t.r[key_] = tag_

            if LST[0] == "p3":
                dbg("h2T", h2T, h2T_t)
                raise _Stop()
            AM_.reset()
            wslotB_fi = AM_.alloc([128, 8, 2, 768], BF16, "wfiB")
            wslotB_fo = AM_.alloc([128, 6, D], BF16, "wfoB")
            slots = [(wslotA_fi, wslotA_fo), (wslotB_fi, wslotB_fo)]
            AS_.reset(0)
            gTs.append(AS_.alloc([128, 6, 512], BF16, "gT1"))
            if not last:
                make_wm_slots()
                munits = mod_units(l + 1)
            else:
                munits = []
            ct_i = [0]
            opend = []
            blk_i = 0
            for p in range(4):
                j0, j1 = PARTS[p]
                nj = j1 - j0
                (fi, fi_t), (fo, fo_t) = slots[p % 2]
                yo = None
                if last and p == 3:
                    for f_ in opend:
                        f_()
                    opend = []
                    AW_.reset(AW_.size - 1920 - 36864)
                    yo = AW_.alloc([128, 8, 512], F32, "yo")
                for bi, (c0, c1, lh, rh) in enumerate(BLOCKS):
                    if bi == 1:
                        if p + 1 < 4:
                            load_part(p + 1, *slots[(p + 1) % 2])
                        if (not last) and p == 3:
                            AW_.reset(45056)
                            WIN_NEXT[0] = AW_.alloc([128, 8, 2112], BF16, "win")
                            load_win(l + 1, *WIN_NEXT[0])
                    V = c1 - c0
                    W = V + lh + rh
                    ci = 0 if bi < 5 else 1
                    wcols = slice(c0 - lh, c1 + rh)
                    xb, xb_t = xgs[blk_i % 2]
                    gT, gT_t = gTs[blk_i % 2]
                    blk_i += 1
                    if p == 0:
                        src, rds = xv(XMID[l]), [XMID_t[l][g] for g in B2G[bi]]
                    else:
                        src, rds = xv(XF[l][p - 1]), [XF_t[l][p - 1][bi]]
                    S.dma("sp", xb[:, :, 0:V], src[:, :, c0:c1], reads=rds, writes=[xb_t, comp(xb_t)])
                    for jl in range(nj):
                        j = j0 + jl
                        res = []
                        for half in range(2):
                            ch = j + half * NJ
                            ps, pst = ps_mm()
                            chain(ps[:, 0:W], pst, [(fi[:, k, half, jl * 128:(jl + 1) * 128], h2T[:, k, wcols]) for k in range(8)], [fi_t, h2T_t, comp(h2T_t)])
                            ct, ct_t = cts[ct_i[0] % 4]; ct_i[0] += 1
                            S.op("act", lambda e, ps=ps, ct=ct, ch=ch: e.activation(ct[:, 0:V], ps[:, lh:lh + V], AF.Identity,
                                                                                  bias=convp[:, 3, ch:ch + 1], scale=convp[:, 1, ch:ch + 1]),
                                 reads=[pst, convp_t], writes=[ct_t])
                            s_ = 0 if lh else 1
                            S.op("dve", lambda e, ps=ps, ct=ct, ch=ch, s_=s_: e.scalar_tensor_tensor(
                                ct[:, s_:V], ps[:, lh + s_ - 1:lh + V - 1], convp[:, 0, ch:ch + 1], ct[:, s_:V], ALU.mult, ALU.add),
                                reads=[pst, convp_t, ct_t], writes=[ct_t])
                            e_ = 0 if rh else 1
                            S.op("dve", lambda e, ps=ps, ct=ct, ch=ch, e_=e_: e.scalar_tensor_tensor(
                                ct[:, 0:V - e_], ps[:, lh + 1:lh + 1 + V - e_], convp[:, 2, ch:ch + 1], ct[:, 0:V - e_], ALU.mult, ALU.add),
                                reads=[pst, convp_t, ct_t], writes=[ct_t])
                            res.append((ct, ct_t))
                        (cg, cg_t), (cv, cv_t) = res
                        sg, sg_t = sgs[jl % 2]
                        S.op("act", lambda e, cg=cg, sg=sg: e.activation(sg[:, 0:V], cg[:, 0:V], AF.Silu), reads=[cg_t], writes=[sg_t])
                        S.op("pool", lambda e, sg=sg, cv=cv, jl=jl, gT=gT: e.tensor_tensor(gT[:, jl, 0:V], sg[:, 0:V], cv[:, 0:V], ALU.mult),
                             reads=[sg_t, cv_t], writes=[gT_t])

                    def ffn_out(p=p, bi=bi, c0=c0, c1=c1, V=V, ci=ci, xb=xb, xb_t=xb_t, gT=gT, gT_t=gT_t, fo=fo, fo_t=fo_t, nj=nj, yo=yo):
                        for o in range(8):
                            ps, pst = ps_acc()
                            chain(ps[:, 0:V], pst, [(fo[:, jl, o * 128:(o + 1) * 128], gT[:, jl, 0:V]) for jl in range(nj)], [fo_t, gT_t])
                            S.op("dve", lambda e, ps=ps, o=o: e.scalar_tensor_tensor(xb[:, o, 0:V], ps[:, 0:V], md[:, 40 + o, ci:ci + 1], xb[:, o, 0:V], ALU.mult, ALU.add),
                                 reads=[pst, md_t, xb_t], writes=[xb_t])
                        if p < 3:
                            S.dma("sp", xv(XF[l][p])[:, :, c0:c1], xb[:, :, 0:V], reads=[xb_t], writes=[XF_t[l][p][bi]])
                        elif not last:
                            S.dma("sp", xv(XL[l + 1])[:, :, c0:c1], xb[:, :, 0:V], reads=[xb_t], writes=[XL_t[l + 1][bi]])
                        else:
                            yo_, yo_t = yo
                            norm_group(xb[:, :, 0:V], xb_t, V, gfin, None, yo_, yo_t, sq2, sq2_t, rs2, rs2_t, [gfin_t])
                            S.dma("sp", xv(yT_d)[:, :, c0:c1], yo_[:, :, 0:V], reads=[yo_t])
                    for f_ in opend:
                        f_()
                    opend = [ffn_out]
                    if munits:
                        munits.pop(0)()
            for f_ in opend:
                f_()
            while munits:
                munits.pop(0)()
        AW_.reset(45056)
        WIN_NEXT[0] = AW_.alloc([128, 8, 2112], BF16, "win")
        load_win(0, *WIN_NEXT[0])
        make_wm_slots()
        AW_.reset(0)
        WM_SLOTS.append(AW_.alloc([128, 8, 256], BF16, "wm2"))
        WM_SLOTS.append(AW_.alloc([128, 8, 256], BF16, "wm3"))
        for u in mod_units(0):
            u()
        try:
            if stop == "mod":
                dbg("md", mods[0][0], mods[0][1]); dbg("A1", A1s[0][0], A1s[0][1]); dbg("A2", A2s[0][0], A2s[0][1])
                raise _Stop()
            for l in range(depth):
                layer(l)
        except _Stop:
            pass
        S.finish("sp")
    return nc


_NC_CACHE = {}


def _rope_tables():
    t = np.arange(NS)
    nf = 8
    inv = (np.float32(10000.0) ** (-np.arange(nf, dtype=np.float32) / np.float32(nf))).astype(np.float32)
    ang_r = (t // 64).astype(np.float32)[:, None] * inv
    ang_c = (t % 64).astype(np.float32)[:, None] * inv
    C = np.zeros((32, NS), np.float32)
    Sg = np.zeros((32, NS), np.float32)
    for o, ang in ((0, ang_r), (16, ang_c)):
        C[o:o + 8] = np.cos(ang).T
        C[o + 8:o + 16] = np.cos(ang).T
        Sg[o:o + 8] = -np.sin(ang).T
        Sg[o + 8:o + 16] = np.sin(ang).T
    return C, Sg


def _na_tables(na_rpb):
    traw = np.zeros((L, 4, 128, 5, 10, 64), np.float32)
    mask = np.full((128, 5, 10, 64), MASKV, np.float32)
    qc = np.arange(64); kc = np.arange(64)
    cs = np.clip(qc - 8, 0, 48)
    colvalid = (kc[None, :] >= cs[:, None]) & (kc[None, :] < cs[:, None] + 16)
    coloff = np.clip(kc[None, :] - qc[:, None], -15, 15) + 15
    for v in range(5):
        base = NA_BASE[v]
        for ql in range(2):
            for j in range(10):
                dr = base + j - ql
                if v == 0:
                    rv = -4 <= dr <= 3
                else:
                    qr = {1: 0, 2: 2, 3: 28, 4: 30}[v] + ql
                    rs = min(max(qr - 4, 0), 24)
                    rv = rs <= qr + dr < rs + 8
                if -7 <= dr <= 7:
                    traw[:, :, ql * 64:(ql + 1) * 64, v, j, :] = na_rpb[:, :, dr + 7, :][:, :, coloff]
                if rv:
                    mask[ql * 64:(ql + 1) * 64, v, j, :] = np.where(colvalid, np.float32(0.0), np.float32(MASKV))
    return traw.reshape(L, 4, 128, 3200), mask.reshape(128, 3200)


def _colT(v, n):
    return np.ascontiguousarray(v.reshape(n, 128).T)


def _prep_shared(inp):
    f = lambda a: np.ascontiguousarray(np.asarray(a, dtype=np.float32))
    sh = {}
    sh["w_mod"] = f(inp["w_mod"])
    sh["b_modT"] = f(np.stack([_colT(np.asarray(inp["b_mod"])[l], 48) for l in range(L)]))
    sh["gvec"] = f(np.stack([np.concatenate([_colT(np.asarray(inp["g_mix"])[l], 8), _colT(np.asarray(inp["g_ffn"])[l], 8)], 1) for l in range(L)]))
    sh["g_fin"] = f(_colT(np.asarray(inp["g_final"]), 8))
    w_in = np.asarray(inp["w_in"])
    perm = np.array(list(range(8, 16)) + list(range(0, 8)) + list(range(24, 32)) + list(range(16, 24)))
    cols = np.concatenate([np.arange(0, 256), np.arange(256, 512), np.arange(768, 1152), np.arange(1152, 1408),
                           np.arange(1344, 1440), np.arange(1344, 1408), 1408 + perm,
                           np.arange(1440, 1696), np.arange(512, 768), np.arange(1696, 1952)])
    assert cols.shape[0] == 2112
    sh["w_in2"] = f(w_in[:, :, cols])
    w_uq = np.asarray(inp["w_uq"])
    cu = []
    for h in range(8):
        cu.append(np.arange(96 * h, 96 * h + 96))
        cu.append(np.concatenate([np.arange(96 * h, 96 * h + 64), 96 * h + 64 + perm]))
    sh["w_uq2"] = f(w_uq[:, :, np.concatenate(cu)])
    sh["g_cqT"] = f(np.stack([_colT(np.asarray(inp["g_cq"])[l], 3) for l in range(L)]))
    w_ukv = np.asarray(inp["w_ukv"])
    ck = np.concatenate([np.arange(128 * h, 128 * h + 64) for h in range(8)] + [np.arange(128 * h + 64, 128 * h + 128) for h in range(8)])
    sh["w_ukv2"] = f(w_ukv[:, :, ck])
    sh["g_ckvT"] = f(np.stack([_colT(np.asarray(inp["g_ckv"])[l], 2) for l in range(L)]))
    sh["g_sgu_b"] = f(np.broadcast_to(np.asarray(inp["g_sgu"])[:, None, :], (L, 128, 256)))
    sh["w_sguT"] = f(np.asarray(inp["w_sgu"]).transpose(0, 3, 1, 2))
    b_sgu = np.asarray(inp["b_sgu"])
    bb = np.zeros((L, 128, 2, 128), np.float32)
    for pr in range(2):
        for q in range(2):
            bb[:, q * 64:(q + 1) * 64, pr, :] = b_sgu[:, 2 * pr + q, None, :]
    sh["b_sgu_b"] = bb
    sh["w_out"] = f(inp["w_out"])
    sh["w_ffn_in"] = f(inp["w_ffn_in"])
    cw = np.asarray(inp["ffn_conv_w"]); cb = np.asarray(inp["ffn_conv_b"])
    cp = np.zeros((L, 128, 4, 44), np.float32)
    for l in range(L):
        for i in range(3):
            cp[l, :, i, :] = _colT(cw[l, i], 44)
        cp[l, :, 3, :] = _colT(cb[l], 44)
    sh["convp"] = cp
    sh["w_ffn_out"] = f(inp["w_ffn_out"])
    traw, mask = _na_tables(np.asarray(inp["na_rpb"]))
    sh["na_traw"] = f(traw); sh["na_mask"] = f(mask)
    C, Sg = _rope_tables()
    sh["ropeC"] = C; sh["ropeS"] = Sg
    sh["ident"] = np.eye(128, dtype=np.float32)
    return sh


def kernel(x_prompt, x_sample, cache_na_k, cache_na_v, cache_mla_ckv, cache_mla_krope, c, c_ctx,
           w_mod, b_mod, g_mix, w_in, na_rpb, g_cq, w_uq, g_ckv, w_ukv, g_sgu, w_sgu, b_sgu,
           w_out, g_ffn, w_ffn_in, ffn_conv_w, ffn_conv_b, w_ffn_out, g_final, _depth=L, _trace=False, _stop=None, _cores=None):
    inp = dict(w_mod=w_mod, b_mod=b_mod, g_mix=g_mix, w_in=w_in, na_rpb=na_rpb, g_cq=g_cq, w_uq=w_uq, g_ckv=g_ckv,
               w_ukv=w_ukv, g_sgu=g_sgu, w_sgu=w_sgu, b_sgu=b_sgu, w_out=w_out, g_ffn=g_ffn, w_ffn_in=w_ffn_in,
               ffn_conv_w=ffn_conv_w, ffn_conv_b=ffn_conv_b, w_ffn_out=w_ffn_out, g_final=g_final)
    sh = _prep_shared(inp)
    x_prompt = np.asarray(x_prompt, np.float32); x_sample = np.asarray(x_sample, np.float32)
    cnk = np.asarray(cache_na_k, np.float32); cnv = np.asarray(cache_na_v, np.float32)
    cckv = np.asarray(cache_mla_ckv, np.float32); ckr = np.asarray(cache_mla_krope, np.float32)
    c = np.asarray(c, np.float32); c_ctx = np.asarray(c_ctx, np.float32)
    in_maps = []
    for core in range(8):
        b = core // 2
        xp = x_prompt[2 * core:2 * core + 2].reshape(512, D)
        xT = np.ascontiguousarray(np.concatenate([x_sample[b], xp], 0).T)
        cond = np.stack([c[b], c_ctx], 0)
        m = dict(sh)
        m["xT"] = xT
        m["condT"] = np.ascontiguousarray(cond.reshape(2, 8, 128).transpose(2, 1, 0))
        m["c_na_kT"] = np.ascontiguousarray(cnk[b].reshape(L, 256, 256).transpose(0, 2, 1))
        m["c_na_v"] = np.ascontiguousarray(cnv[b].reshape(L, 256, 256))
        m["c_ckvT"] = np.ascontiguousarray(cckv[b].transpose(0, 2, 1))
        m["c_krT"] = np.ascontiguousarray(ckr[b].transpose(0, 2, 1))
        in_maps.append(m)
    key = (_depth, _stop)
    if key not in _NC_CACHE:
        _NC_CACHE[key] = build_nc(_depth, _stop)
    nc = _NC_CACHE[key]
    if _cores is not None:
        in_maps = in_maps[:_cores]
    res = run_bass_kernel_spmd(nc, in_maps, core_ids=list(range(len(in_maps))), **({"trace": True} if _trace else {}))
    R = res.results
    if _stop is not None:
        return R
    y_prompt = np.zeros((16, 256, D), np.float32)
    y_sample = np.zeros((4, NS, D), np.float32)
    nk = np.zeros((16, L, 256, 4, 64), np.float32)
    nv = np.zeros((16, L, 256, 4, 64), np.float32)
    nckv = np.zeros((16, L, 256, 256), np.float32)
    nkr = np.zeros((16, L, 256, 32), np.float32)
    for core in range(8):
        r = R[core]
        yT = r["yT"]
        if core % 2 == 0:
            y_sample[core // 2] = yT[:, 0:NS].T
        for i in range(2):
            sidx = 2 * core + i
            cs = slice(256 * i, 256 * (i + 1))
            y_prompt[sidx] = yT[:, NS + 256 * i:NS + 256 * (i + 1)].T
            nk[sidx] = r["o_kT"][:, :, cs].transpose(0, 2, 1).reshape(L, 256, 4, 64)
            nv[sidx] = r["o_v"][:, cs, :].reshape(L, 256, 4, 64)
            nckv[sidx] = r["o_ckvT"][:, :, cs].transpose(0, 2, 1)
            nkr[sidx] = r["o_krT"][:, :, cs].transpose(0, 2, 1)
    if _trace:
        kernel.last_res = res
    return (y_prompt, y_sample, nk, nv, nckv, nkr)
```

```python
import os
import numpy as np
import concourse.bass as bass
import concourse.mybir as mybir
from concourse.bass_utils import run_bass_kernel_spmd
from contextlib import ExitStack

F32 = mybir.dt.float32
BF16 = mybir.dt.bfloat16
ALU = mybir.AluOpType
AF = mybir.ActivationFunctionType
AX = mybir.AxisListType


class T:
    __slots__ = ("name", "w", "r", "excl")

    def __init__(self, name, excl=False):
        self.name = name
        self.w = []
        self.r = {}
        self.excl = excl


class Sched:
    EPOCH = 16000

    def __init__(self, nc, es, n_dma_slots=12, strict_same=True):
        self.nc = nc
        self.es = es
        self.eng = {"pe": nc.tensor, "act": nc.scalar, "dve": nc.vector, "pool": nc.gpsimd, "sp": nc.sync}
        self.cnt = {e: 0 for e in self.eng}
        self.sems = {e: [] for e in self.eng}
        self.waited = {e: {} for e in self.eng}
        self.strict_same = strict_same
        self.slots = []
        self.qslots = {}
        self.qrr = {}
        for q in ("sp", "pool", "act"):
            ids = []
            for i in range(n_dma_slots):
                sem = es.enter_context(nc.semaphore("d_%s_%d" % (q, i)))
                self.slots.append([sem, 0])
                ids.append(len(self.slots) - 1)
            self.qslots[q] = ids
            self.qrr[q] = 0
        self.nwaits = 0

    def _esem(self, e, seq):
        k = (seq - 1) // self.EPOCH
        while len(self.sems[e]) <= k:
            self.sems[e].append(self.es.enter_context(self.nc.semaphore("s_%s_%d" % (e, len(self.sems[e])))))
        return self.sems[e][k], (seq - 1) % self.EPOCH + 1

    def _wait(self, e, deps):
        need = {}
        for d in deps:
            if d is None:
                continue
            if d[0] == "e":
                _, e2, s2 = d
                if e2 == e and (e == "pe" or not self.strict_same):
                    continue
                key = e2
            else:
                _, sid, s2 = d
                key = ("d", sid)
            if self.waited[e].get(key, 0) >= s2:
                continue
            if need.get(key, 0) < s2:
                need[key] = s2
        for key, s2 in need.items():
            if isinstance(key, tuple):
                sem = self.slots[key[1]][0]
                self.eng[e].wait_ge(sem, 16 * s2)
            else:
                sem, val = self._esem(key, s2)
                self.eng[e].wait_ge(sem, val)
            self.waited[e][key] = s2
            self.nwaits += 1

    def _deps(self, reads, writes, dma=None):
        deps = []
        for t in reads:
            deps.extend(t.w)
        for t in writes:
            if dma and not t.r and t.w and all(x[0] == "d" and x[1] in self.qslots[dma] for x in t.w):
                continue
            deps.extend(t.w)
            deps.extend(t.r.values())
        return deps

    def op(self, e, fn, reads=(), writes=(), inc=True):
        deps = self._deps(reads, writes)
        for t in reads:
            if t.excl:
                deps.extend(v for k, v in t.r.items() if k != e)
        self._wait(e, deps)
        ins = fn(self.eng[e])
        seq = self.cnt[e] + 1
        if inc:
            self.cnt[e] = seq
            sem, _ = self._esem(e, seq)
            ins.then_inc(sem, 1)
        tag = ("e", e, seq)
        for t in reads:
            t.r[e] = tag
        for t in writes:
            t.w = [tag]
            t.r = {}
        return ins

    def dma(self, q, out, in_, reads=(), writes=(), **kw):
        ids = self.qslots[q]
        sid = ids[self.qrr[q]]
        self.qrr[q] = (self.qrr[q] + 1) % len(ids)
        slot = self.slots[sid]
        deps = self._deps(reads, writes, dma=q)
        if slot[1] > 0:
            deps.append(("d", sid, slot[1]))
        self._wait(q, deps)
        ins = self.eng[q].dma_start(out=out, in_=in_, **kw)
        slot[1] += 1
        ins.then_inc(slot[0], 16)
        tag = ("d", sid, slot[1])
        for t in reads:
            t.r[("d", sid)] = tag
        for t in writes:
            if not t.r and t.w and all(x[0] == "d" and x[1] in self.qslots[q] for x in t.w):
                t.w = t.w + [tag]
            else:
                t.w = [tag]
            t.r = {}
        return ins

    def finish(self, e="sp"):
        deps = []
        for sid, slot in enumerate(self.slots):
            if slot[1] > 0:
                deps.append(("d", sid, slot[1]))
        for e2 in self.eng:
            if e2 != e and self.cnt[e2] > 0:
                deps.append(("e", e2, self.cnt[e2]))
        self._wait(e, deps)


D = 1024; L = 4; NT = 2560; NS = 2048; KC = 2816
DFF = 2816; NJ = 22
EPS = 1e-6
MLA_SCALE = 96 ** -0.5
MASKV = -30000.0
GROUPS = [(0, 512, 0), (512, 512, 0), (1024, 512, 0), (1536, 512, 0), (2048, 512, 1)]
BLOCKS = [(0, 510, 0, 1), (510, 1020, 1, 1), (1020, 1530, 1, 1), (1530, 2040, 1, 1), (2040, 2048, 1, 0),
          (2048, 2304, 0, 0), (2304, 2560, 0, 0)]
G2B = {0: [0, 1], 1: [1, 2], 2: [2, 3], 3: [3, 4], 4: [5, 6]}
B2G = {0: [0], 1: [0, 1], 2: [1, 2], 3: [2, 3], 4: [3], 5: [4], 6: [4]}
PARTS = [(0, 4), (4, 10), (10, 16), (16, 22)]
NA_BASE = [-4, 0, -2, -4, -6]


def na_variant(qt):
    return {0: 1, 1: 2, 14: 3, 15: 4}.get(qt, 0)


def na_tiles(qt):
    r0 = 2 * qt
    rs0 = min(max(r0 - 4, 0), 24)
    rs1 = min(max(r0 + 1 - 4, 0), 24)
    return list(range(rs0 // 2, (rs1 + 7) // 2 + 1))


COMP = {}


def comp(t):
    if id(t) not in COMP:
        COMP[id(t)] = (t, T(t.name + "_hi"))
    return COMP[id(t)][1]


class Arena:
    def __init__(self, nc, base, size, name):
        self.nc = nc; self.base = base; self.size = size; self.name = name
        self.off = 0; self.live = []; self.n = 0

    def reset(self, off=0):
        self.off = off

    def alloc(self, shape, dt, name="t"):
        nb = int(np.prod(shape[1:])) * (4 if dt == F32 else 2)
        nb = (nb + 63) // 64 * 64
        s = self.off; e = s + nb
        assert e <= self.size, (self.name, name, e, self.size)
        self.off = e
        self.n += 1
        h = self.nc.alloc_sbuf_tensor_at("%s_%s_%d" % (self.name, name, self.n), list(shape), dt, offset=self.base + s)
        t = T(name)
        keep = []
        for (s2, e2, t2) in self.live:
            if s2 < e and s < e2:
                tags = list(t2.r.values()) + list(t2.w)
                if id(t2) in COMP:
                    c2 = COMP[id(t2)][1]
                    tags += list(c2.r.values()) + list(c2.w)
                for tag in tags:
                    key = tag[1] if tag[0] == "e" else ("d", tag[1])
                    if key not in t.r or t.r[key][2] < tag[2]:
                        t.r[key] = tag
                if s2 < s:
                    keep.append((s2, s, t2))
                if e < e2:
                    keep.append((e, e2, t2))
            else:
                keep.append((s2, e2, t2))
        keep.append((s, e, t))
        self.live = keep
        return h.ap(), t


class _Stop(Exception):
    pass


def build_nc(depth=L, stop=None):
    nc = bass.Bass("TRN2", target_bir_lowering=False)
    es = ExitStack()
    with es:
        es.enter_context(nc.allow_low_precision("bf16 matmul operands, fp32 accumulation"))
        es.enter_context(nc.allow_non_contiguous_dma("layout"))
        S = Sched(nc, es)

        def din(name, shape):
            return nc.dram_tensor(name, list(shape), F32, kind="ExternalInput").ap()

        def dout(name, shape):
            return nc.dram_tensor(name, list(shape), F32, kind="ExternalOutput").ap()

        def dscr(name, shape):
            return nc.dram_tensor(name, list(shape), F32, kind="Internal").ap()

        xT_d = din("xT", [D, NT])
        condT_d = din("condT", [128, 8, 2])
        wmod_d = din("w_mod", [L, D, 6 * D])
        bmodT_d = din("b_modT", [L, 128, 48])
        gvec_d = din("gvec", [L, 128, 16])
        gfin_d = din("g_fin", [128, 8])
        win_d = din("w_in2", [L, D, 2112])
        wuq_d = din("w_uq2", [L, 384, 1536])
        gcq_d = din("g_cqT", [L, 128, 3])
        wukv_d = din("w_ukv2", [L, 256, 1024])
        gckv_d = din("g_ckvT", [L, 128, 2])
        gsgu_d = din("g_sgu_b", [L, 128, 256])
        wsgu_d = din("w_sguT", [L, 128, 4, 128])
        bsgu_d = din("b_sgu_b", [L, 128, 2, 128])
        wout_d = din("w_out", [L, D, D])
        wfi_d = din("w_ffn_in", [L, D, 2 * DFF])
        convp_d = din("convp", [L, 128, 4, 44])
        wfo_d = din("w_ffn_out", [L, DFF, D])
        cnak_d = din("c_na_kT", [L, 256, 256])
        cnav_d = din("c_na_v", [L, 256, 256])
        cckv_d = din("c_ckvT", [L, 256, 256])
        ckr_d = din("c_krT", [L, 32, 256])
        traw_d = din("na_traw", [L, 4, 128, 3200])
        mask_d = din("na_mask", [128, 3200])
        ropeC_d = din("ropeC", [32, NS])
        ropeS_d = din("ropeS", [32, NS])
        ident_d = din("ident", [128, 128])

        yT_d = dout("yT", [D, NT])
        okT_d = dout("o_kT", [L, 256, 512])
        ov_d = dout("o_v", [L, 512, 256])
        ockv_d = dout("o_ckvT", [L, 256, 512])
        okr_d = dout("o_krT", [L, 32, 512])

        XL = [xT_d] + [dscr("xl%d" % i, [D, NT]) for i in range(1, depth)]
        XMID = [dscr("xmid%d" % i, [D, NT]) for i in range(depth)]
        XF = [[dscr("xf%d_%d" % (i, p), [D, NT]) for p in range(3)] for i in range(depth)]
        XL_t = [[T("xl") for _ in range(7)] for _ in range(depth)]
        XMID_t = [[T("xm") for _ in range(5)] for _ in range(depth)]
        XF_t = [[[T("xf") for _ in range(7)] for p in range(3)] for _ in range(depth)]

        def xv(d):
            return d.rearrange("(k p) n -> p k n", p=128)

        BASE = 16512
        AP_ = Arena(nc, BASE, 16384, "P")
        AS_ = Arena(nc, BASE + 16384, 71680, "S")
        AM_ = Arena(nc, BASE + 16384 + 71680, 40960, "M")
        AW_ = Arena(nc, BASE + 16384 + 71680 + 40960, 212864 - 16384 - 71680 - 40960, "W")

        PS = []
        for i in range(8):
            PS.append((nc.alloc_psum_tensor("ps%d" % i, [128, 512], F32).ap(), T("ps%d" % i, excl=True)))
        rr = {"mm": 0, "acc": 0}

        def ps_mm():
            i = rr["mm"]; rr["mm"] = (i + 1) % 4
            return PS[i]

        def ps_mm6():
            i = rr.get("mm6", 0); rr["mm6"] = (i + 1) % 6
            return PS[i]

        def ps_acc():
            i = rr["acc"]; rr["acc"] = (i + 1) % 2
            return PS[4 + i]
        PSN = PS[6]
        PSMOD = PS[7]

        def chain(ps, pst, pairs, reads):
            n = len(pairs)
            for i, (l_, r_) in enumerate(pairs):
                S.op("pe", lambda e, l_=l_, r_=r_, i=i: e.matmul(ps, l_, r_, start=(i == 0), stop=(i == n - 1)),
                     reads=reads, writes=[pst], inc=(i == n - 1))

        ident, ident_t = AP_.alloc([128, 128], BF16, "ident")
        ones, ones_t = AP_.alloc([128, 128], BF16, "ones")
        ropeC, ropeC_t = AP_.alloc([128, NS], BF16, "ropeC")
        ropeS, ropeS_t = AP_.alloc([128, NS], BF16, "ropeS")
        csil, csil_t = AP_.alloc([128, 8, 2], BF16, "csil")
        condf, condf_t = AP_.alloc([128, 8, 2], F32, "condf")
        gfin, gfin_t = AP_.alloc([128, 8], F32, "gfin")
        mods = [AP_.alloc([128, 48, 2], F32, "mods%d" % i) for i in range(2)]
        A1s = [AP_.alloc([128, 8, 2], F32, "A1_%d" % i) for i in range(2)]
        A2s = [AP_.alloc([128, 8, 2], F32, "A2_%d" % i) for i in range(2)]
        bmod, bmod_t = AP_.alloc([128, 48], F32, "bmod")
        gvec, gvec_t = AP_.alloc([128, 16], F32, "gvec")
        gcq, gcq_t = AP_.alloc([128, 3], F32, "gcq")
        gckv, gckv_t = AP_.alloc([128, 2], F32, "gckv")
        convp, convp_t = AP_.alloc([128, 4, 44], F32, "convp")
        bsgu, bsgu_t = AP_.alloc([128, 2, 128], F32, "bsgu")
        gsgu, gsgu_t = AP_.alloc([128, 256], F32, "gsgu")
        wsgu, wsgu_t = AP_.alloc([128, 4, 128], BF16, "wsgu")
        small, small_t = AP_.alloc([128, 16], F32, "small")
        rsx, rsx_t = AP_.alloc([128, 512], F32, "rsx")

        S.dma("pool", ident, ident_d, writes=[ident_t])
        S.op("dve", lambda e: e.memset(ones, 1.0), writes=[ones_t])
        S.dma("pool", ropeC[64:96, :], ropeC_d, writes=[ropeC_t])
        S.dma("pool", ropeS[64:96, :], ropeS_d, writes=[ropeS_t])
        S.dma("sp", condf, condT_d, writes=[condf_t])
        S.dma("sp", gfin, gfin_d, writes=[gfin_t])
        S.op("act", lambda e: e.activation(csil, condf, AF.Silu), reads=[condf_t], writes=[csil_t])
        S.op("dve", lambda e: e.tensor_scalar(gfin, gfin, 32.0, None, ALU.mult), reads=[gfin_t], writes=[gfin_t])

        WM_SLOTS = []

        def make_wm_slots():
            WM_SLOTS.clear()
            AM_.reset(AM_.size - 4096)
            WM_SLOTS.append(AM_.alloc([128, 8, 256], BF16, "wm0"))
            AS_.reset(AS_.size - 4096)
            WM_SLOTS.append(AS_.alloc([128, 8, 256], BF16, "wm1"))

        def mod_units(l):
            par = l % 2
            md, md_t = mods[par]
            A1, A1_t = A1s[par]
            A2, A2_t = A2s[par]
            psm, psm_t = PSMOD
            units = []

            ns = len(WM_SLOTS)

            def load(j):
                wm, wm_t = WM_SLOTS[j % ns]
                S.dma("pool", wm, wmod_d[l].rearrange("(k p) n -> p k n", p=128)[:, :, 256 * j:256 * (j + 1)], writes=[wm_t])

            def blk(j):
                wm, wm_t = WM_SLOTS[j % ns]
                for oc in range(2):
                    m = 2 * j + oc
                    chain(psm[:, 2 * m:2 * m + 2], psm_t,
                          [(wm[:, k, oc * 128:(oc + 1) * 128], csil[:, k, :]) for k in range(8)], [wm_t, csil_t])
                if j + ns < 24:
                    load(j + ns)

            def fin():
                S.dma("sp", bmod, bmodT_d[l], writes=[bmod_t])
                S.dma("sp", gvec, gvec_d[l], writes=[gvec_t])
                S.op("dve", lambda e: e.tensor_tensor(md, psm[:, 0:96].rearrange("p (m c) -> p m c", c=2),
                                                       bmod.unsqueeze(2).broadcast_to([128, 48, 2]), ALU.add),
                     reads=[psm_t, bmod_t], writes=[md_t])
                S.op("dve", lambda e: e.tensor_scalar(gvec, gvec, 32.0, None, ALU.mult), reads=[gvec_t], writes=[gvec_t])
                for (A, A_t, c0, g0) in ((A1, A1_t, 8, 0), (A2, A2_t, 32, 8)):
                    S.op("dve", lambda e, A=A, c0=c0: e.tensor_scalar(A, md[:, c0:c0 + 8, :], 1.0, None, ALU.add),
                         reads=[md_t], writes=[A_t])
                    S.op("dve", lambda e, A=A, g0=g0: e.tensor_tensor(A, A, gvec[:, g0:g0 + 8].unsqueeze(2).broadcast_to([128, 8, 2]), ALU.mult),
                         reads=[A_t, gvec_t], writes=[A_t])
            units.append(lambda: [load(j_) for j_ in range(ns)])
            for j in range(24):
                units.append(lambda j=j: blk(j))
            units.append(fin)
            return units

        def norm_group(x, x_t, n, A, Bv, out, out_t, sq, sq_t, rs, rs_t, vec_reads, nk=8, eps_n=1024.0):
            psn, psn_t = PSN
            S.op("act", lambda e: e.activation(sq[:, 0:nk, 0:n], x, AF.Square), reads=[x_t], writes=[sq_t])
            chain(psn[:, 0:n], psn_t, [(ones, sq[:, k, 0:n]) for k in range(nk)], [ones_t, sq_t])
            S.op("act", lambda e: e.activation(rs[:, 0:n], psn[:, 0:n], AF.Sqrt, bias=eps_n * EPS), reads=[psn_t], writes=[rs_t])
            S.op("dve", lambda e: e.reciprocal(rs[:, 0:n], rs[:, 0:n]), reads=[rs_t], writes=[rs_t])
            S.op("pool", lambda e: e.tensor_tensor(x, x, rs[:, 0:n].unsqueeze(1).broadcast_to([128, nk, n]), ALU.mult),
                 reads=[x_t, rs_t], writes=[x_t])
            for k in range(nk):
                if Bv is None:
                    S.op("act", lambda e, k=k: e.activation(out[:, k, 0:n], x[:, k, :], AF.Identity, scale=A[:, k:k + 1]),
                         reads=[x_t] + vec_reads, writes=[out_t])
                else:
                    S.op("act", lambda e, k=k: e.activation(out[:, k, 0:n], x[:, k, :], AF.Identity, bias=Bv[:, k:k + 1], scale=A[:, k:k + 1]),
                         reads=[x_t] + vec_reads, writes=[out_t])

        def norm_stages(x, x_t, n, A, Bv, out, out_t, sq, sq_t, rs, rs_t, vec_reads, psb, mode=0):
            psn, psn_t = psb
            x_hi = comp(x_t); out_hi = comp(out_t)
            st = []
            st.append(lambda: S.op("act", lambda e: e.activation(sq[:, 0:8, 0:n], x, AF.Square), reads=[x_t, x_hi], writes=[sq_t]))
            st.append(lambda: chain(psn[:, 0:n], psn_t, [(ones, sq[:, k, 0:n]) for k in range(8)], [ones_t, sq_t]))

            def s3():
                S.op("act", lambda e: e.activation(rs[:, 0:n], psn[:, 0:n], AF.Sqrt, bias=1024.0 * EPS), reads=[psn_t], writes=[rs_t])
                S.op("dve", lambda e: e.reciprocal(rs[:, 0:n], rs[:, 0:n]), reads=[rs_t], writes=[rs_t])
                if mode == 1:
                    S.op("pool", lambda e: e.tensor_tensor(x, x, rs[:, 0:n].unsqueeze(1).broadcast_to([128, 8, n]), ALU.mult),
                         reads=[x_t, x_hi, rs_t], writes=[x_t, x_hi])
                    return
                S.op("dve", lambda e: e.tensor_tensor(x[:, 0:4, :], x[:, 0:4, :], rs[:, 0:n].unsqueeze(1).broadcast_to([128, 4, n]), ALU.mult),
                     reads=[x_t, rs_t], writes=[x_t])
                S.op("pool", lambda e: e.tensor_tensor(x[:, 4:8, :], x[:, 4:8, :], rs[:, 0:n].unsqueeze(1).broadcast_to([128, 4, n]), ALU.mult),
                     reads=[x_hi, rs_t], writes=[x_hi])
            st.append(s3)

            def s4(k0, k1):
                for k in range(k0, k1):
                    eng, xt_, ot_ = ("dve", x_t, out_t) if k < 4 else ("pool", x_hi, out_hi)
                    if mode == 1 or (mode == 2 and k < 4):
                        S.op("act", lambda e, k=k: e.activation(out[:, k, 0:n], x[:, k, :], AF.Identity, bias=Bv[:, k:k + 1], scale=A[:, k:k + 1]),
                             reads=[xt_] + vec_reads, writes=[ot_])
                        continue
                    S.op(eng, lambda e, k=k: e.tensor_scalar(out[:, k, 0:n], x[:, k, :], A[:, k:k + 1], Bv[:, k:k + 1], ALU.mult, ALU.add),
                         reads=[xt_] + vec_reads, writes=[ot_])
            st.append(lambda: (s4(0, 2), s4(4, 6)))
            st.append(lambda: (s4(2, 4), s4(6, 8)))
            return st

        def gelu(dst, dst_t, src, src_t, t1, t1_t, t2, t2_t, eng2="pool"):
            S.op("act", lambda e: e.activation(dst, src, AF.Gelu_apprx_tanh), reads=[src_t], writes=[dst_t])
            return
            S.op("act", lambda e: e.activation(t1, src, AF.Square), reads=[src_t], writes=[t1_t])
            S.op("dve", lambda e: e.tensor_scalar(t1, t1, 0.044715, 1.0, ALU.mult, ALU.add), reads=[t1_t], writes=[t1_t])
            S.op("dve", lambda e: e.tensor_tensor(t1, t1, src, ALU.mult), reads=[t1_t, src_t], writes=[t1_t])
            S.op("act", lambda e: e.activation(t2, t1, AF.Sigmoid, scale=1.5957691216057308), reads=[t1_t], writes=[t2_t])
            S.op("dve", lambda e: e.tensor_tensor(dst, t2, src, ALU.mult), reads=[t2_t, src_t], writes=[dst_t])

        DBG = {}

        def dbg(name, ap, t):
            shp = list(ap.shape)
            d_ = nc.dram_tensor("dbg_" + name, shp, ap.dtype, kind="ExternalOutput").ap()
            S.dma("sp", d_, ap, reads=[t])
            DBG[name] = d_

        LST = [None]

        WIN_NEXT = [None]

        def load_win(l, win, win_t):
            wv = win_d[l].rearrange("(k p) n -> p k n", p=128)
            for c0 in range(0, 2112, 528):
                S.dma("pool", win[:, :, c0:c0 + 528], wv[:, :, c0:c0 + 528], writes=[win_t])

        def cp(tag):
            if LST[0] == "p1:" + tag:
                raise _Stop()

        def layer(l):
            LST[0] = (stop[3:] if (stop and stop.startswith("L%d:" % l)) else (stop if (l == 0 and stop and not stop.startswith("L")) else None))
            par = l % 2
            md, md_t = mods[par]
            A1, A1_t = A1s[par]
            A2, A2_t = A2s[par]
            last = (l == depth - 1)

            S.dma("sp", gcq, gcq_d[l], writes=[gcq_t])
            S.dma("sp", gckv, gckv_d[l], writes=[gckv_t])
            S.dma("sp", convp, convp_d[l], writes=[convp_t])
            S.dma("sp", bsgu, bsgu_d[l], writes=[bsgu_t])
            S.dma("sp", gsgu, gsgu_d[l], writes=[gsgu_t])
            S.dma("pool", wsgu, wsgu_d[l], writes=[wsgu_t])
            S.op("dve", lambda e: e.tensor_scalar(gcq, gcq, float(np.sqrt(384.0)), None, ALU.mult), reads=[gcq_t], writes=[gcq_t])
            S.op("dve", lambda e: e.tensor_scalar(gckv, gckv, 16.0, None, ALU.mult), reads=[gckv_t], writes=[gckv_t])
            S.op("dve", lambda e: e.tensor_scalar(gsgu, gsgu, 16.0, None, ALU.mult), reads=[gsgu_t], writes=[gsgu_t])

            AS_.reset()
            qaT, qaT_t = AS_.alloc([128, 2, NT], BF16, "qaT")
            kaT, kaT_t = AS_.alloc([128, 2, KC], BF16, "kaT")
            Vna, Vna_t = AS_.alloc([128, 22, 384], BF16, "Vna")
            cqnT, cqnT_t = AS_.alloc([128, 3, NT], BF16, "cqnT")
            ckvnT, ckvnT_t = AS_.alloc([128, 2, KC], BF16, "ckvnT")
            krT, krT_t = AS_.alloc([128, KC], BF16, "krT")
            mixT = nc.alloc_sbuf_tensor_at("mixT_%d" % l, [128, 8, NT], BF16, offset=AM_.base).ap()
            AM_.reset()
            sq, sq_t = AM_.alloc([128, 8, 512], BF16, "sq")
            cqf, cqf_t = AM_.alloc([128, 3, 512], F32, "cqf")
            sqc, sqc_t = AM_.alloc([128, 3, 512], BF16, "sqc")
            g1, g1_t = AM_.alloc([128, 512], F32, "g1")
            g2, g2_t = AM_.alloc([128, 512], F32, "g2")
            uT, uT_t = AM_.alloc([128, 2, 512], F32, "uT")
            rs, rs_t = AM_.alloc([128, 512], F32, "rs")
            assert AM_.off <= 6 * NT * 2
            AM_.reset(6 * NT * 2)
            _, mixC_t = AM_.alloc([128, 2, NT], BF16, "mixC")

            if WIN_NEXT[0] is not None:
                win, win_t = WIN_NEXT[0]
                WIN_NEXT[0] = None
            else:
                AW_.reset(45056)
                win, win_t = AW_.alloc([128, 8, 2112], BF16, "win")
                load_win(l, win, win_t)
            AW_.reset(45056 + 33792)
            vnpad, vnpad_t = AW_.alloc([128, 4, 512], BF16, "vnpad")
            AW_.reset()
            stg = [AW_.alloc([128, 512], F32, "stg%d" % i) for i in range(3)]
            xg, xg_t = AW_.alloc([128, 8, 512], F32, "xg")
            hgs = [AW_.alloc([128, 8, 512], BF16, "hg%d" % i) for i in range(2)]
            ty4, ty4_t = AW_.alloc([128, 4, 256], F32, "ty4")
            tq, tq_t = AW_.alloc([128, 256], F32, "tq")
            assert AW_.off <= 45056
            stg_i = [0]

            def stage():
                i = stg_i[0]; stg_i[0] = (i + 1) % len(stg)
                return stg[i]

            for c in range(2):
                S.dma("pool", kaT[:, c, 0:256], cnak_d[l][c * 128:(c + 1) * 128, :], writes=[kaT_t])
                S.dma("pool", ckvnT[:, c, 0:256], cckv_d[l][c * 128:(c + 1) * 128, :], writes=[ckvnT_t])
            S.dma("pool", krT[64:96, 0:256], ckr_d[l], writes=[krT_t])
            S.op("pool", lambda e: e.memset(Vna.rearrange("p t (a b) -> p t a b", b=192)[:, :, :, 64:128], 1.0), writes=[Vna_t])
            for t_ in range(2):
                src = cnav_d[l][t_ * 128:(t_ + 1) * 128, :].rearrange("p (a h d) -> p a h d", a=2, h=2)
                dstv = Vna[:, t_, :].rearrange("p (a b) -> p a b", b=192)
                S.dma("pool", dstv[:, :, 0:64], src[:, :, 0, :], writes=[Vna_t])
                S.dma("pool", dstv[:, :, 128:192], src[:, :, 1, :], writes=[Vna_t])
            S.op("pool", lambda e: e.memset(vnpad, 0.0), writes=[vnpad_t])

            cp("ld")
            xsrc = xv(XL[l])

            def p1_norm(gi):
                c0_, n_, ci_ = GROUPS[gi]
                xreads = [] if l == 0 else [XL_t[l][b] for b in G2B[gi]]
                S.dma("sp", xg, xsrc[:, :, c0_:c0_ + n_], reads=xreads, writes=[xg_t, comp(xg_t)])
                hg_, hg_t_ = hgs[gi % 2]
                return norm_stages(xg, xg_t, n_, A1[:, :, ci_], md[:, 0:8, ci_], hg_, hg_t_, sq, sq_t, rsx, rsx_t, [A1_t, md_t], PSMOD)
            for f_ in p1_norm(0):
                f_()
            sgu_pend = []
            for gi, (c0, n, ci) in enumerate(GROUPS):
                cols = slice(c0, c0 + n)
                kcols = slice(256 + c0, 256 + c0 + n)
                hg, hg_t = hgs[gi % 2]
                nxt = p1_norm(gi + 1) if gi + 1 < len(GROUPS) else []

                def nstage():
                    if nxt:
                        nxt.pop(0)()
                rd = [win_t, hg_t, comp(hg_t)]

                def fm(col0, M):
                    ps, pst = ps_mm6()
                    chain(ps[0:M, 0:n], pst, [(win[:, k, col0:col0 + M], hg[:, k, :]) for k in range(8)], rd)
                    return ps, pst
                for c in range(2):
                    ps, pst = fm(c * 128, 128)
                    S.op("act", lambda e, ps=ps, c=c: e.activation(qaT[:, c, cols], ps[:, 0:n], AF.Copy, scale=0.125),
                         reads=[pst], writes=[qaT_t])
                for c in range(2):
                    ps, pst = fm(256 + c * 128, 128)
                    S.op("dve", lambda e, ps=ps, c=c: e.tensor_copy(kaT[:, c, kcols], ps[:, 0:n]), reads=[pst], writes=[kaT_t])
                    if ci == 1:
                        st, st_t = stage()
                        S.op("act", lambda e, ps=ps, st=st: e.copy(st[:, 0:n], ps[:, 0:n]), reads=[pst], writes=[st_t])
                        S.dma("sp", okT_d[l][c * 128:(c + 1) * 128, :], st[:, 0:n], reads=[st_t])
                nstage()
                def nrm_proj(colb, nch, gv, gv_t):
                    for c in range(nch):
                        ps, pst = fm(colb + c * 128, 128)
                        S.op("act", lambda e, ps=ps, c=c, gv=gv: e.activation(cqf[:, c, 0:n], ps[:, 0:n], AF.Copy, scale=gv[:, c:c + 1]),
                             reads=[pst, gv_t], writes=[cqf_t])
                        S.op("act", lambda e, ps=ps, c=c: e.activation(sqc[:, c, 0:n], ps[:, 0:n], AF.Square), reads=[pst], writes=[sqc_t])

                def nrm_fin(nch, epsn, dstT, dst_t, dcols, is_ckv):
                    psn, psn_t = PSN
                    chain(psn[:, 0:n], psn_t, [(ones, sqc[:, c, 0:n]) for c in range(nch)], [ones_t, sqc_t])
                    S.op("act", lambda e, epsn=epsn: e.activation(rs[:, 0:n], psn[:, 0:n], AF.Sqrt, bias=epsn * EPS), reads=[psn_t], writes=[rs_t])
                    S.op("dve", lambda e: e.reciprocal(rs[:, 0:n], rs[:, 0:n]), reads=[rs_t], writes=[rs_t])
                    S.op("dve", lambda e, nch=nch, dstT=dstT, dcols=dcols: e.tensor_tensor(
                        dstT[:, :, dcols], cqf[:, 0:nch, 0:n], rs[:, 0:n].unsqueeze(1).broadcast_to([128, nch, n]), ALU.mult),
                        reads=[cqf_t, rs_t], writes=[dst_t])
                    if is_ckv and ci == 1:
                        for c in range(2):
                            st, st_t = stage()
                            S.op("pool", lambda e, st=st, c=c: e.tensor_tensor(st[:, 0:n], cqf[:, c, 0:n], rs[:, 0:n], ALU.mult),
                                 reads=[cqf_t, rs_t], writes=[st_t])
                            S.dma("sp", ockv_d[l][c * 128:(c + 1) * 128, :], st[:, 0:n], reads=[st_t])

                def krope():
                    psA, psA_t = fm(1152, 96)
                    if ci == 0:
                        psB, psB_t = fm(1248, 96)
                        S.op("dve", lambda e, psA=psA: e.tensor_tensor(g1[64:96, 0:n], psA[64:96, 0:n], ropeC[64:96, cols], ALU.mult),
                             reads=[psA_t, ropeC_t], writes=[g1_t])
                        S.op("dve", lambda e, psB=psB: e.tensor_tensor(g2[64:96, 0:n], psB[64:96, 0:n], ropeS[64:96, cols], ALU.mult),
                             reads=[psB_t, ropeS_t], writes=[g2_t])
                        S.op("pool", lambda e: e.tensor_tensor(krT[64:96, kcols], g1[64:96, 0:n], g2[64:96, 0:n], ALU.add),
                             reads=[g1_t, g2_t], writes=[krT_t])
                    else:
                        S.op("dve", lambda e, psA=psA: e.tensor_copy(krT[64:96, kcols], psA[64:96, 0:n]), reads=[psA_t], writes=[krT_t])
                        st, st_t = stage()
                        S.op("act", lambda e, psA=psA, st=st: e.copy(st[64:96, 0:n], psA[64:96, 0:n]), reads=[psA_t], writes=[st_t])
                        S.dma("sp", okr_d[l], st[64:96, 0:n], reads=[st_t])

                def uproj():
                    for c in range(2):
                        ps, pst = fm(1344 + c * 128, 128)
                        gelu(uT[:, c, 0:n], uT_t, ps[:, 0:n], pst, g1[:, 0:n], g1_t, g2[:, 0:n], g2_t)

                if sgu_pend:
                    sgu_pend.pop(0)()
                nrm_proj(512, 3, gcq, gcq_t)
                nstage()
                krope()
                nrm_fin(3, 384.0, cqnT, cqnT_t, cols, False)
                nstage()
                nrm_proj(896, 2, gckv, gckv_t)
                nstage()
                uproj()
                nrm_fin(2, 256.0, ckvnT, ckvnT_t, kcols, True)
                nstage()
                for tt in range(4):
                    ps, pst = ps_mm6()
                    chain(ps, pst, [(hg[:, k, tt * 128:(tt + 1) * 128], win[:, k, 1600:2112]) for k in range(8)], rd)
                    kt = 2 + (c0 // 128) + tt
                    dstv = Vna[:, kt, :].rearrange("p (a b) -> p a b", b=192)
                    srcv = ps[:, 0:256].rearrange("p (a h d) -> p a h d", a=2, h=2)
                    S.op("act", lambda e, dstv=dstv, srcv=srcv: e.copy(dstv[:, :, 0:64], srcv[:, :, 0, :]), reads=[pst], writes=[Vna_t])
                    S.op("act", lambda e, dstv=dstv, srcv=srcv: e.copy(dstv[:, :, 128:192], srcv[:, :, 1, :]), reads=[pst], writes=[Vna_t])
                    if ci == 1:
                        st, st_t = stage()
                        S.op("dve", lambda e, ps=ps, st=st: e.tensor_copy(st[:, 0:256], ps[:, 0:256]), reads=[pst], writes=[st_t])
                        S.dma("sp", ov_d[l][tt * 128:(tt + 1) * 128, :], st[:, 0:256], reads=[st_t])
                    S.op("act", lambda e, ps=ps, tt=tt: e.activation(ty4[:, tt, :], ps[:, 256:512], AF.Gelu_apprx_tanh), reads=[pst], writes=[ty4_t])
                    S.op("act", lambda e, tt=tt: e.activation(tq, ty4[:, tt, :], AF.Square, accum_out=small[:, tt:tt + 1]), reads=[ty4_t], writes=[tq_t, small_t])
                S.op("act", lambda e: e.activation(small[:, 4:8], small[:, 0:4], AF.Sqrt, bias=256.0 * EPS), reads=[small_t], writes=[small_t])
                S.op("dve", lambda e: e.reciprocal(small[:, 4:8], small[:, 4:8]), reads=[small_t], writes=[small_t])
                for tt in range(4):
                    vp = vnpad[:, tt, :].rearrange("p (a b) -> p a b", b=256)
                    yv = ty4[:, tt, :].rearrange("p (a b) -> p a b", b=128)
                    gv2 = gsgu.rearrange("p (a b) -> p a b", b=128)
                    S.op("dve", lambda e, vp=vp, yv=yv, gv2=gv2, tt=tt: e.scalar_tensor_tensor(vp[:, :, 0:64], yv[:, :, 0:64], small[:, 4 + tt:5 + tt], gv2[:, :, 0:64], ALU.mult, ALU.mult),
                         reads=[ty4_t, small_t, gsgu_t], writes=[vnpad_t])
                    S.op("dve", lambda e, vp=vp, yv=yv, gv2=gv2, tt=tt: e.scalar_tensor_tensor(vp[:, :, 192:256], yv[:, :, 64:128], small[:, 4 + tt:5 + tt], gv2[:, :, 64:128], ALU.mult, ALU.mult),
                         reads=[ty4_t, small_t, gsgu_t], writes=[vnpad_t])
                while nxt:
                    nstage()
                def sgu(cols=cols):
                    for pr in range(2):
                        ps, pst = ps_mm6()
                        for tt in range(4):
                            for q in range(2):
                                gq = 2 * pr + q
                                S.op("pe", lambda e, ps=ps, tt=tt, gq=gq, q=q: e.matmul(
                                    ps[:, tt * 128:(tt + 1) * 128], vnpad[:, tt, gq * 128:(gq + 1) * 128], wsgu[:, gq, :],
                                    start=(q == 0), stop=(q == 1)), reads=[vnpad_t, wsgu_t], writes=[pst])
                        S.op("dve", lambda e, ps=ps, pr=pr: e.tensor_tensor(
                            g1.rearrange("p (a b) -> p a b", b=128), ps.rearrange("p (a b) -> p a b", b=128),
                            bsgu[:, pr, :].unsqueeze(1).broadcast_to([128, 4, 128]), ALU.add), reads=[pst, bsgu_t], writes=[g1_t])
                        S.op("pool", lambda e, pr=pr: e.tensor_tensor(mixT[:, 6 + pr, cols], g1, uT[:, pr, :], ALU.mult),
                             reads=[g1_t, uT_t], writes=[mixC_t])
                sgu_pend.append(sgu)

            AW_.reset()
            tabs = [AW_.alloc([128, 5, 640], BF16, "tab%d" % i) for i in range(1)]
            maskb, maskb_t = AW_.alloc([128, 3200], BF16, "maskb")
            S.dma("pool", maskb, mask_d, writes=[maskb_t])

            tab_ts = [T("tabv%d" % v) for v in range(5)]
            for v_ in range(5):
                for tag_ in list(tabs[0][1].r.values()) + list(tabs[0][1].w):
                    key_ = tag_[1] if tag_[0] == "e" else ("d", tag_[1])
                    tab_ts[v_].r[key_] = tag_

            def load_tab(h, vs=(1, 2, 0, 3, 4)):
                tab = tabs[0][0]
                for v_ in vs:
                    S.dma("pool", tab[:, v_, :], traw_d[l, h][:, v_ * 640:(v_ + 1) * 640], writes=[tab_ts[v_]])
                    S.op("pool", lambda e, v_=v_: e.tensor_tensor(tab[:, v_, :], tab[:, v_, :], maskb[:, v_ * 640:(v_ + 1) * 640], ALU.add),
                         reads=[tab_ts[v_], maskb_t], writes=[tab_ts[v_]])
            load_tab(0)
            while sgu_pend:
                sgu_pend.pop(0)()
            cp("end")
            if LST[0] == "p1":
                dbg("qaT", qaT, qaT_t); dbg("kaT", kaT, kaT_t); dbg("Vna", Vna, Vna_t); dbg("cqnT", cqnT, cqnT_t)
                dbg("ckvnT", ckvnT, ckvnT_t); dbg("krT", krT, krT_t); dbg("mixC", mixT[:, 6:8, :], mixC_t)
                raise _Stop()
            AM_.reset()
            _, mixA_t = AM_.alloc([128, 6, NT], BF16, "mixAB")
            mixB_t = mixA_t
            AW_.reset(12800)
            PTs = [AW_.alloc([128, 512], BF16, "PT%d" % i) for i in range(4)]
            rcs = [AW_.alloc([128, 512], F32, "rc%d" % i) for i in range(1)]
            wuq, wuq_t = AW_.alloc([128, 3, 1536], BF16, "wuq")
            wukv, wukv_t = AW_.alloc([128, 2, 1024], BF16, "wukv")
            Vmla, Vmla_t = AW_.alloc([128, 22, 768], BF16, "Vmla")
            Khs = [AW_.alloc([128, KC], BF16, "Kh%d" % i) for i in range(2)]
            Qhs = [AW_.alloc([128, 512], BF16, "Qh%d" % i) for i in range(2)]
            qt1, qt1_t = AW_.alloc([128, 512], F32, "qt1")
            qt2, qt2_t = AW_.alloc([128, 512], F32, "qt2")
            pt_i = [0]; rc_i = [0]

            def nextPT():
                i = pt_i[0]; pt_i[0] = (i + 1) % 4
                return PTs[i]

            def nextrc():
                i = rc_i[0]; rc_i[0] = (i + 1) % 1
                return rcs[i]

            def normalize(psO, psO_t, po, n, dst, dst_t):
                rc, rc_t = nextrc()
                dr = slice(64 - po, 128 - po)
                orr = slice(po, po + 64)
                S.op("dve", lambda e: e.reciprocal(rc[dr, 0:n], psO[dr, 0:n]), reads=[psO_t], writes=[rc_t])
                S.op("dve", lambda e: e.tensor_tensor(dst, psO[orr, 0:n], rc[dr, 0:n], ALU.mult), reads=[psO_t, rc_t], writes=[dst_t])

            SEQS = [(qg * 512, 512, list(range(18)), True) for qg in range(4)] + \
                   [(NS, 256, [18, 19], False), (NS + 256, 256, [20, 21], False)]
            def build_K(h):
                Kh, Kh_t = Khs[h % 2]
                for kg in range(0, KC, 512):
                    n = min(512, KC - kg)
                    ps, pst = ps_mm()
                    chain(ps[0:64, 0:n], pst, [(wukv[:, kc, 64 * h:64 * h + 64], ckvnT[:, kc, kg:kg + n]) for kc in range(2)], [wukv_t, ckvnT_t])
                    S.op("dve", lambda e, ps=ps, n=n, kg=kg: e.tensor_copy(Kh[0:64, kg:kg + n], ps[0:64, 0:n]), reads=[pst], writes=[Kh_t])
                S.op("pool", lambda e: e.tensor_copy(Kh[64:96, :], krT[64:96, :]), reads=[krT_t], writes=[Kh_t])

            def build_Q(h, si):
                q0, n, kts, rope = SEQS[si]
                qcols = slice(q0, q0 + n)
                Qh, Qh_t = Qhs[(h * 6 + si) % len(Qhs)]
                psA, psA_t = PSN
                chain(psA[0:96, 0:n], psA_t, [(wuq[:, kc, h * 192:h * 192 + 96], cqnT[:, kc, qcols]) for kc in range(3)], [wuq_t, cqnT_t])
                S.op("dve", lambda e: e.tensor_copy(Qh[0:64, 0:n], psA[0:64, 0:n]), reads=[psA_t], writes=[Qh_t])
                if rope:
                    psB, psB_t = PSMOD
                    chain(psB[0:96, 0:n], psB_t, [(wuq[:, kc, h * 192 + 96:h * 192 + 192], cqnT[:, kc, qcols]) for kc in range(3)], [wuq_t, cqnT_t])
                    S.op("dve", lambda e: e.tensor_tensor(qt1[64:96, 0:n], psA[64:96, 0:n], ropeC[64:96, qcols], ALU.mult),
                         reads=[psA_t, ropeC_t], writes=[qt1_t])
                    S.op("dve", lambda e: e.tensor_tensor(qt2[64:96, 0:n], psB[64:96, 0:n], ropeS[64:96, qcols], ALU.mult),
                         reads=[psB_t, ropeS_t], writes=[qt2_t])
                    S.op("pool", lambda e: e.tensor_tensor(Qh[64:96, 0:n], qt1[64:96, 0:n], qt2[64:96, 0:n], ALU.add),
                         reads=[qt1_t, qt2_t], writes=[Qh_t])
                else:
                    S.op("dve", lambda e: e.tensor_copy(Qh[64:96, 0:n], psA[64:96, 0:n]), reads=[psA_t], writes=[Qh_t])

            ORDER = [(h_, si_) for h_ in range(8) for si_ in range(len(SEQS))]
            nbuilt = [0]
            pro_done = [False]

            def build_next(cur):
                if nbuilt[0] < len(ORDER) and nbuilt[0] <= cur + 5:
                    build_Q(*ORDER[nbuilt[0]])
                    nbuilt[0] += 1

            def mla_prologue():
                pro_done[0] = True
                AW_.reset(6400)
                for i_ in range(4):
                    Qhs.append(AW_.alloc([128, 512], BF16, "Qhx%d" % i_))
                build_K(0)
                build_next(0)
                build_next(0)
            S.dma("pool", wukv, wukv_d[l].rearrange("(k p) n -> p k n", p=128), writes=[wukv_t])
            S.dma("pool", wuq, wuq_d[l].rearrange("(k p) n -> p k n", p=128), writes=[wuq_t])
            S.op("pool", lambda e: e.memset(Vmla.rearrange("p t (a b) -> p t a b", b=192)[:, :, :, 64:128], 1.0), writes=[Vmla_t])
            vb_next = [0]

            def vbuild(cnt):
                for _ in range(cnt):
                    kt = vb_next[0]
                    if kt >= 22:
                        return
                    vb_next[0] += 1
                    ps, pst = ps_mm()
                    chain(ps, pst, [(ckvnT[:, kc, kt * 128:(kt + 1) * 128], wukv[:, kc, 512:1024]) for kc in range(2)], [ckvnT_t, wukv_t])
                    dstv = Vmla[:, kt, :].rearrange("p (a b) -> p a b", b=192)
                    srcv = ps.rearrange("p (a h d) -> p a h d", a=4, h=2)
                    S.op("dve", lambda e, dstv=dstv, srcv=srcv: e.tensor_copy(dstv[:, :, 0:64], srcv[:, :, 0, :]), reads=[pst], writes=[Vmla_t])
                    S.op("dve", lambda e, dstv=dstv, srcv=srcv: e.tensor_copy(dstv[:, :, 128:192], srcv[:, :, 1, :]), reads=[pst], writes=[Vmla_t])
            pend = []
            for h in range(4):
                pr = h // 2; po = (h % 2) * 64
                prs = slice(po, po + 64)
                tab, tab_t = tabs[0]
                if h > 0:
                    load_tab(h, (0, 3, 4))
                    vbuild(5)
                for qg in range(4):
                    psO, psO_t = ps_acc()
                    for qq in range(4):
                        qt = qg * 4 + qq
                        qcols = slice(qt * 128, (qt + 1) * 128)
                        v = na_variant(qt)
                        blocks = [("c", 0), ("c", 1)] + [("l", kt) for kt in na_tiles(qt)]
                        banks = []
                        for b0 in range(0, len(blocks), 4):
                            sub = blocks[b0:b0 + 4]
                            ps, pst = ps_mm()
                            for bi, (kind, kt) in enumerate(sub):
                                dst = ps[:, bi * 128:(bi + 1) * 128]
                                if kind == "c":
                                    S.op("pe", lambda e, dst=dst, kt=kt: e.matmul(dst, kaT[prs, pr, kt * 128:(kt + 1) * 128], qaT[prs, pr, qcols], start=True, stop=True),
                                         reads=[kaT_t, qaT_t], writes=[pst])
                                else:
                                    j = (2 * kt - 2 * qt) - NA_BASE[v]
                                    S.op("pe", lambda e, dst=dst, kt=kt: e.matmul(dst, kaT[prs, pr, 256 + kt * 128:256 + (kt + 1) * 128], qaT[prs, pr, qcols], start=True, stop=False),
                                         reads=[kaT_t, qaT_t], writes=[pst], inc=False)
                                    S.op("pe", lambda e, dst=dst, j=j: e.matmul(dst, tab[:, v, j * 64:j * 64 + 128], ident, start=False, stop=True),
                                         reads=[tab_ts[v], ident_t], writes=[pst])
                            PT, PT_t = nextPT()
                            w = len(sub) * 128
                            S.op("act", lambda e, ps=ps, PT=PT, w=w: e.activation(PT[:, 0:w], ps[:, 0:w], AF.Exp), reads=[pst], writes=[PT_t])
                            banks.append((PT, PT_t, sub))
                        def pv(banks=banks, psO=psO, psO_t=psO_t, qq=qq, nb=len(blocks), pr=pr, po=po):
                            cnt = 0
                            for (PT, PT_t, sub) in banks:
                                for bi, (kind, kt) in enumerate(sub):
                                    vt = kt if kind == "c" else 2 + kt
                                    S.op("pe", lambda e, PT=PT, bi=bi, vt=vt, cnt=cnt: e.matmul(
                                        psO[:, qq * 128:(qq + 1) * 128], Vna[:, vt, pr * 192 + po:pr * 192 + po + 128], PT[:, bi * 128:(bi + 1) * 128],
                                        start=(cnt == 0), stop=(cnt == nb - 1)), reads=[Vna_t, PT_t], writes=[psO_t])
                                    cnt += 1
                        for f_ in pend:
                            f_()
                        pend = [pv]
                        if qq == 3:
                            pend.append(lambda psO=psO, psO_t=psO_t, po=po, prs=prs, pr=pr, qg=qg:
                                        normalize(psO, psO_t, po, 512, mixT[prs, pr, qg * 512:(qg + 1) * 512], mixA_t))
                            if h >= 1:
                                vbuild(1)
                            if h == 3 and qg == 2:
                                mla_prologue()
                            if qg == 0 and h + 1 < 4:
                                load_tab(h + 1, (1, 2))
            for f_ in pend:
                f_()
            for sq_i in range(2):
                qcols = slice(NS + sq_i * 256, NS + (sq_i + 1) * 256)
                for h in range(4):
                    pr = h // 2; po = (h % 2) * 64
                    prs = slice(po, po + 64)
                    ps, pst = ps_mm()
                    for b in range(2):
                        kc0 = 256 + NS + sq_i * 256 + b * 128
                        S.op("pe", lambda e, ps=ps, b=b, kc0=kc0: e.matmul(ps[:, b * 256:(b + 1) * 256], kaT[prs, pr, kc0:kc0 + 128], qaT[prs, pr, qcols], start=True, stop=True),
                             reads=[kaT_t, qaT_t], writes=[pst])
                    PT, PT_t = nextPT()
                    S.op("act", lambda e, ps=ps, PT=PT: e.activation(PT, ps, AF.Exp), reads=[pst], writes=[PT_t])
                    psO, psO_t = ps_acc()
                    for b in range(2):
                        vt = 18 + sq_i * 2 + b
                        S.op("pe", lambda e, PT=PT, b=b, vt=vt: e.matmul(psO[:, 0:256], Vna[:, vt, pr * 192 + po:pr * 192 + po + 128], PT[:, b * 256:(b + 1) * 256],
                                                                         start=(b == 0), stop=(b == 1)), reads=[Vna_t, PT_t], writes=[psO_t])
                    normalize(psO, psO_t, po, 256, mixT[prs, pr, qcols], mixA_t)

            for v_ in range(5):
                for tag_ in list(tab_ts[v_].r.values()) + list(tab_ts[v_].w):
                    key_ = tag_[1] if tag_[0] == "e" else ("d", tag_[1])
                    if key_ not in tabs[0][1].r or tabs[0][1].r[key_][2] < tag_[2]:
                        tabs[0][1].r[key_] = tag_
            AS_.reset(0)
            wout, wout_t = AS_.alloc([128, 8, D], BF16, "wout")
            wout_ts = [T("wout%d" % o) for o in range(8)]
            for o in range(8):
                for tag_ in list(wout_t.r.values()) + list(wout_t.w):
                    key_ = tag_[1] if tag_[0] == "e" else ("d", tag_[1])
                    wout_ts[o].r[key_] = tag_
                S.dma("pool", wout[:, :, o * 128:(o + 1) * 128], wout_d[l].rearrange("(k p) n -> p k n", p=128)[:, :, o * 128:(o + 1) * 128], writes=[wout_ts[o]])
            if LST[0] == "na":
                dbg("mixA", mixT[:, 0:2, :], mixA_t)
                raise _Stop()
            vbuild(22)
            LA = 3
            mpend = []
            if not pro_done[0]:
                mla_prologue()
            for h in range(8):
                pr = h // 2; po = (h % 2) * 64
                prs = slice(po, po + 64)
                Kh, Kh_t = Khs[h % 2]
                for si, (q0, n, kts, rope) in enumerate(SEQS):
                    qcols = slice(q0, q0 + n)
                    Qh, Qh_t = Qhs[(h * 6 + si) % len(Qhs)]
                    cur = h * 6 + si
                    while nbuilt[0] <= cur:
                        build_next(cur)
                    psO, psO_t = ps_acc()
                    nk = len(kts)
                    for i, kt in enumerate(kts):
                        ps, pst = ps_mm()
                        S.op("pe", lambda e, ps=ps, kt=kt: e.matmul(ps[:, 0:n], Kh[0:96, kt * 128:(kt + 1) * 128], Qh[0:96, 0:n], start=True, stop=True),
                             reads=[Kh_t, Qh_t], writes=[pst])
                        PT, PT_t = nextPT()
                        S.op("act", lambda e, ps=ps, PT=PT: e.activation(PT[:, 0:n], ps[:, 0:n], AF.Exp, scale=MLA_SCALE), reads=[pst], writes=[PT_t])

                        def pv(PT=PT, PT_t=PT_t, kt=kt, i=i, nk=nk, psO=psO, psO_t=psO_t, n=n, qcols=qcols, pr=pr, po=po, prs=prs):
                            S.op("pe", lambda e: e.matmul(psO[:, 0:n], Vmla[:, kt, pr * 192 + po:pr * 192 + po + 128], PT[:, 0:n],
                                                          start=(i == 0), stop=(i == nk - 1)), reads=[Vmla_t, PT_t], writes=[psO_t])
                            if i == nk - 1:
                                normalize(psO, psO_t, po, n, mixT[prs, 2 + pr, qcols], mixB_t)
                        mpend.append(pv)
                        if len(mpend) > LA:
                            mpend.pop(0)()
                        if nk > 4 and i in (3, 7, 11):
                            build_next(cur)
                        if si == 1 and i == 14 and h + 1 < 8:
                            build_K(h + 1)
            for f_ in mpend:
                f_()

            if LST[0] == "mla":
                dbg("mixA", mixT[:, 0:6, :], mixA_t); dbg("Vmla", Vmla, Vmla_t); dbg("Kh", Khs[0][0], Khs[0][1])
                raise _Stop()
            AS_.reset(16384)
            h2T, h2T_t = AS_.alloc([128, 8, NT], BF16, "h2T")
            gTs = [AS_.alloc([128, 6, 512], BF16, "gT%d" % i) for i in range(1)]
            sq2, sq2_t = AS_.alloc([128, 8, 512], BF16, "sq2")
            AW_.reset()
            xgs = [AW_.alloc([128, 8, 512], F32, "xg%d" % i) for i in range(2)]
            rs2, rs2_t = AW_.alloc([128, 512], F32, "rs2")
            cts = [AW_.alloc([128, 512], F32, "ct%d" % i) for i in range(4)]
            sgs = [AW_.alloc([128, 512], BF16, "sg%d" % i) for i in range(2)]
            wslotA_fi = AW_.alloc([128, 8, 2, 768], BF16, "wfiA")
            wslotA_fo = AW_.alloc([128, 6, D], BF16, "wfoA")

            def load_part(p, slot_fi, slot_fo):
                j0, j1 = PARTS[p]
                nj = j1 - j0
                fi, fi_t = slot_fi
                fo, fo_t = slot_fo
                wv_ = wfi_d[l].rearrange("(k p) n -> p k n", p=128)
                S.dma("pool", fi[:, :, 0, 0:nj * 128], wv_[:, :, 128 * j0:128 * j1], writes=[fi_t])
                S.dma("pool", fi[:, :, 1, 0:nj * 128], wv_[:, :, DFF + 128 * j0:DFF + 128 * j1], writes=[fi_t])
                S.dma("pool", fo[:, 0:nj, :], wfo_d[l][128 * j0:128 * j1, :].rearrange("(k p) n -> p k n", p=128), writes=[fo_t])
            load_part(0, wslotA_fi, wslotA_fo)

            mix_reads = [mixA_t, mixC_t]
            p3prev = []
            for gi, (c0, n, ci) in enumerate(GROUPS):
                cols = slice(c0, c0 + n)
                xg, xg_t = xgs[gi % 2]
                xreads = [] if l == 0 else [XL_t[l][b] for b in G2B[gi]]
                S.dma("sp", xg, xsrc[:, :, cols], reads=xreads, writes=[xg_t, comp(xg_t)])
                for o in range(8):
                    ps, pst = ps_mm6()
                    chain(ps[:, 0:n], pst, [(wout[:, k, o * 128:(o + 1) * 128], mixT[:, k, cols]) for k in range(8)], [wout_ts[o]] + mix_reads)
                    S.op("dve", lambda e, ps=ps, o=o, xg=xg: e.scalar_tensor_tensor(xg[:, o, :], ps[:, 0:n], md[:, 16 + o, ci:ci + 1], xg[:, o, :], ALU.mult, ALU.add),
                         reads=[pst, md_t, (xg_t if o < 4 else comp(xg_t))], writes=[(xg_t if o < 4 else comp(xg_t))])
                    if p3prev:
                        p3prev.pop(0)()
                while p3prev:
                    p3prev.pop(0)()
                S.dma("sp", xv(XMID[l])[:, :, cols], xg, reads=[xg_t, comp(xg_t)], writes=[XMID_t[l][gi]])
                h2v = h2T[:, :, cols]
                p3prev = norm_stages(xg, xg_t, n, A2[:, :, ci], md[:, 24:32, ci], h2v, h2T_t, sq2, sq2_t, rs2, rs2_t, [A2_t, md_t], PSN, mode=2)
            while p3prev:
                p3prev.pop(0)()
            for o in range(8):
                for tag_ in list(wout_ts[o].r.values()) + list(wout_ts[o].w):
                    key_ = tag_[1] if tag_[0] == "e" else ("d", tag_[1])
                    if key_ not in wout_t.r or wout_t.r[key_][2] < tag_[2]:
                        wout_t.r[key_] = tag_

            if LST[0] == "p3":
                dbg("h2T", h2T, h2T_t)
                raise _Stop()
            AM_.reset()
            wslotB_fi = AM_.alloc([128, 8, 2, 768], BF16, "wfiB")
            wslotB_fo = AM_.alloc([128, 6, D], BF16, "wfoB")
            slots = [(wslotA_fi, wslotA_fo), (wslotB_fi, wslotB_fo)]
            AS_.reset(0)
            gTs.append(AS_.alloc([128, 6, 512], BF16, "gT1"))
            if not last:
                make_wm_slots()
                munits = mod_units(l + 1)
            else:
                munits = []
            ct_i = [0]
            opend = []
            blk_i = 0
            for p in range(4):
                j0, j1 = PARTS[p]
                nj = j1 - j0
                (fi, fi_t), (fo, fo_t) = slots[p % 2]
                yo = None
                if last and p == 3:
                    for f_ in opend:
                        f_()
                    opend = []
                    AW_.reset(AW_.size - 1920 - 36864)
                    yo = AW_.alloc([128, 8, 512], F32, "yo")
                for bi, (c0, c1, lh, rh) in enumerate(BLOCKS):
                    if bi == 1:
                        if p + 1 < 4:
                            load_part(p + 1, *slots[(p + 1) % 2])
                        if (not last) and p == 3:
                            AW_.reset(45056)
                            WIN_NEXT[0] = AW_.alloc([128, 8, 2112], BF16, "win")
                            load_win(l + 1, *WIN_NEXT[0])
                    V = c1 - c0
                    W = V + lh + rh
                    ci = 0 if bi < 5 else 1
                    wcols = slice(c0 - lh, c1 + rh)
                    xb, xb_t = xgs[blk_i % 2]
                    gT, gT_t = gTs[blk_i % 2]
                    blk_i += 1
                    if p == 0:
                        src, rds = xv(XMID[l]), [XMID_t[l][g] for g in B2G[bi]]
                    else:
                        src, rds = xv(XF[l][p - 1]), [XF_t[l][p - 1][bi]]
                    S.dma("sp", xb[:, :, 0:V], src[:, :, c0:c1], reads=rds, writes=[xb_t, comp(xb_t)])
                    for jl in range(nj):
                        j = j0 + jl
                        res = []
                        for half in range(2):
                            ch = j + half * NJ
                            ps, pst = ps_mm()
                            chain(ps[:, 0:W], pst, [(fi[:, k, half, jl * 128:(jl + 1) * 128], h2T[:, k, wcols]) for k in range(8)], [fi_t, h2T_t, comp(h2T_t)])
                            ct, ct_t = cts[ct_i[0] % 4]; ct_i[0] += 1
                            S.op("act", lambda e, ps=ps, ct=ct, ch=ch: e.activation(ct[:, 0:V], ps[:, lh:lh + V], AF.Identity,
                                                                                  bias=convp[:, 3, ch:ch + 1], scale=convp[:, 1, ch:ch + 1]),
                                 reads=[pst, convp_t], writes=[ct_t])
                            s_ = 0 if lh else 1
                            S.op("dve", lambda e, ps=ps, ct=ct, ch=ch, s_=s_: e.scalar_tensor_tensor(
                                ct[:, s_:V], ps[:, lh + s_ - 1:lh + V - 1], convp[:, 0, ch:ch + 1], ct[:, s_:V], ALU.mult, ALU.add),
                                reads=[pst, convp_t, ct_t], writes=[ct_t])
                            e_ = 0 if rh else 1
                            S.op("dve", lambda e, ps=ps, ct=ct, ch=ch, e_=e_: e.scalar_tensor_tensor(
                                ct[:, 0:V - e_], ps[:, lh + 1:lh + 1 + V - e_], convp[:, 2, ch:ch + 1], ct[:, 0:V - e_], ALU.mult, ALU.add),
                                reads=[pst, convp_t, ct_t], writes=[ct_t])
                            res.append((ct, ct_t))
                        (cg, cg_t), (cv, cv_t) = res
                        sg, sg_t = sgs[jl % 2]
                        S.op("act", lambda e, cg=cg, sg=sg: e.activation(sg[:, 0:V], cg[:, 0:V], AF.Silu), reads=[cg_t], writes=[sg_t])
                        S.op("pool", lambda e, sg=sg, cv=cv, jl=jl, gT=gT: e.tensor_tensor(gT[:, jl, 0:V], sg[:, 0:V], cv[:, 0:V], ALU.mult),
                             reads=[sg_t, cv_t], writes=[gT_t])

                    def ffn_out(p=p, bi=bi, c0=c0, c1=c1, V=V, ci=ci, xb=xb, xb_t=xb_t, gT=gT, gT_t=gT_t, fo=fo, fo_t=fo_t, nj=nj, yo=yo):
                        for o in range(8):
                            ps, pst = ps_acc()
                            chain(ps[:, 0:V], pst, [(fo[:, jl, o * 128:(o + 1) * 128], gT[:, jl, 0:V]) for jl in range(nj)], [fo_t, gT_t])
                            S.op("dve", lambda e, ps=ps, o=o: e.scalar_tensor_tensor(xb[:, o, 0:V], ps[:, 0:V], md[:, 40 + o, ci:ci + 1], xb[:, o, 0:V], ALU.mult, ALU.add),
                                 reads=[pst, md_t, xb_t], writes=[xb_t])
                        if p < 3:
                            S.dma("sp", xv(XF[l][p])[:, :, c0:c1], xb[:, :, 0:V], reads=[xb_t], writes=[XF_t[l][p][bi]])
                        elif not last:
                            S.dma("sp", xv(XL[l + 1])[:, :, c0:c1], xb[:, :, 0:V], reads=[xb_t], writes=[XL_t[l + 1][bi]])
                        else:
                            yo_, yo_t = yo
                            norm_group(xb[:, :, 0:V], xb_t, V, gfin, None, yo_, yo_t, sq2, sq2_t, rs2, rs2_t, [gfin_t])
                            S.dma("sp", xv(yT_d)[:, :, c0:c1], yo_[:, :, 0:V], reads=[yo_t])
                    for f_ in opend:
                        f_()
                    opend = [ffn_out]
                    if munits:
                        munits.pop(0)()
            for f_ in opend:
                f_()
            while munits:
                munits.pop(0)()
        AW_.reset(45056)
        WIN_NEXT[0] = AW_.alloc([128, 8, 2112], BF16, "win")
        load_win(0, *WIN_NEXT[0])
        make_wm_slots()
        AW_.reset(0)
        for i_ in range(6):
            WM_SLOTS.append(AW_.alloc([128, 8, 256], BF16, "wmx%d" % i_))
        for u in mod_units(0):
            u()
        try:
            if stop == "mod":
                dbg("md", mods[0][0], mods[0][1]); dbg("A1", A1s[0][0], A1s[0][1]); dbg("A2", A2s[0][0], A2s[0][1])
                raise _Stop()
            for l in range(depth):
                layer(l)
        except _Stop:
            pass
        S.finish("sp")
    return nc


_NC_CACHE = {}


def _rope_tables():
    t = np.arange(NS)
    nf = 8
    inv = (np.float32(10000.0) ** (-np.arange(nf, dtype=np.float32) / np.float32(nf))).astype(np.float32)
    ang_r = (t // 64).astype(np.float32)[:, None] * inv
    ang_c = (t % 64).astype(np.float32)[:, None] * inv
    C = np.zeros((32, NS), np.float32)
    Sg = np.zeros((32, NS), np.float32)
    for o, ang in ((0, ang_r), (16, ang_c)):
        C[o:o + 8] = np.cos(ang).T
        C[o + 8:o + 16] = np.cos(ang).T
        Sg[o:o + 8] = -np.sin(ang).T
        Sg[o + 8:o + 16] = np.sin(ang).T
    return C, Sg


def _na_tables(na_rpb):
    traw = np.zeros((L, 4, 128, 5, 10, 64), np.float32)
    mask = np.full((128, 5, 10, 64), MASKV, np.float32)
    qc = np.arange(64); kc = np.arange(64)
    cs = np.clip(qc - 8, 0, 48)
    colvalid = (kc[None, :] >= cs[:, None]) & (kc[None, :] < cs[:, None] + 16)
    coloff = np.clip(kc[None, :] - qc[:, None], -15, 15) + 15
    for v in range(5):
        base = NA_BASE[v]
        for ql in range(2):
            for j in range(10):
                dr = base + j - ql
                if v == 0:
                    rv = -4 <= dr <= 3
                else:
                    qr = {1: 0, 2: 2, 3: 28, 4: 30}[v] + ql
                    rs = min(max(qr - 4, 0), 24)
                    rv = rs <= qr + dr < rs + 8
                if -7 <= dr <= 7:
                    traw[:, :, ql * 64:(ql + 1) * 64, v, j, :] = na_rpb[:, :, dr + 7, :][:, :, coloff]
                if rv:
                    mask[ql * 64:(ql + 1) * 64, v, j, :] = np.where(colvalid, np.float32(0.0), np.float32(MASKV))
    return traw.reshape(L, 4, 128, 3200), mask.reshape(128, 3200)


def _colT(v, n):
    return np.ascontiguousarray(v.reshape(n, 128).T)


def _prep_shared(inp):
    f = lambda a: np.ascontiguousarray(np.asarray(a, dtype=np.float32))
    sh = {}
    sh["w_mod"] = f(inp["w_mod"])
    sh["b_modT"] = f(np.stack([_colT(np.asarray(inp["b_mod"])[l], 48) for l in range(L)]))
    sh["gvec"] = f(np.stack([np.concatenate([_colT(np.asarray(inp["g_mix"])[l], 8), _colT(np.asarray(inp["g_ffn"])[l], 8)], 1) for l in range(L)]))
    sh["g_fin"] = f(_colT(np.asarray(inp["g_final"]), 8))
    w_in = np.asarray(inp["w_in"])
    perm = np.array(list(range(8, 16)) + list(range(0, 8)) + list(range(24, 32)) + list(range(16, 24)))
    cols = np.concatenate([np.arange(0, 256), np.arange(256, 512), np.arange(768, 1152), np.arange(1152, 1408),
                           np.arange(1344, 1440), np.arange(1344, 1408), 1408 + perm,
                           np.arange(1440, 1696), np.arange(512, 768), np.arange(1696, 1952)])
    assert cols.shape[0] == 2112
    sh["w_in2"] = f(w_in[:, :, cols])
    w_uq = np.asarray(inp["w_uq"])
    cu = []
    for h in range(8):
        cu.append(np.arange(96 * h, 96 * h + 96))
        cu.append(np.concatenate([np.arange(96 * h, 96 * h + 64), 96 * h + 64 + perm]))
    sh["w_uq2"] = f(w_uq[:, :, np.concatenate(cu)])
    sh["g_cqT"] = f(np.stack([_colT(np.asarray(inp["g_cq"])[l], 3) for l in range(L)]))
    w_ukv = np.asarray(inp["w_ukv"])
    ck = np.concatenate([np.arange(128 * h, 128 * h + 64) for h in range(8)] + [np.arange(128 * h + 64, 128 * h + 128) for h in range(8)])
    sh["w_ukv2"] = f(w_ukv[:, :, ck])
    sh["g_ckvT"] = f(np.stack([_colT(np.asarray(inp["g_ckv"])[l], 2) for l in range(L)]))
    sh["g_sgu_b"] = f(np.broadcast_to(np.asarray(inp["g_sgu"])[:, None, :], (L, 128, 256)))
    sh["w_sguT"] = f(np.asarray(inp["w_sgu"]).transpose(0, 3, 1, 2))
    b_sgu = np.asarray(inp["b_sgu"])
    bb = np.zeros((L, 128, 2, 128), np.float32)
    for pr in range(2):
        for q in range(2):
            bb[:, q * 64:(q + 1) * 64, pr, :] = b_sgu[:, 2 * pr + q, None, :]
    sh["b_sgu_b"] = bb
    sh["w_out"] = f(inp["w_out"])
    sh["w_ffn_in"] = f(inp["w_ffn_in"])
    cw = np.asarray(inp["ffn_conv_w"]); cb = np.asarray(inp["ffn_conv_b"])
    cp = np.zeros((L, 128, 4, 44), np.float32)
    for l in range(L):
        for i in range(3):
            cp[l, :, i, :] = _colT(cw[l, i], 44)
        cp[l, :, 3, :] = _colT(cb[l], 44)
    sh["convp"] = cp
    sh["w_ffn_out"] = f(inp["w_ffn_out"])
    traw, mask = _na_tables(np.asarray(inp["na_rpb"]))
    sh["na_traw"] = f(traw); sh["na_mask"] = f(mask)
    C, Sg = _rope_tables()
    sh["ropeC"] = C; sh["ropeS"] = Sg
    sh["ident"] = np.eye(128, dtype=np.float32)
    return sh


def kernel(x_prompt, x_sample, cache_na_k, cache_na_v, cache_mla_ckv, cache_mla_krope, c, c_ctx,
           w_mod, b_mod, g_mix, w_in, na_rpb, g_cq, w_uq, g_ckv, w_ukv, g_sgu, w_sgu, b_sgu,
           w_out, g_ffn, w_ffn_in, ffn_conv_w, ffn_conv_b, w_ffn_out, g_final, _depth=L, _trace=False, _stop=None, _cores=None):
    inp = dict(w_mod=w_mod, b_mod=b_mod, g_mix=g_mix, w_in=w_in, na_rpb=na_rpb, g_cq=g_cq, w_uq=w_uq, g_ckv=g_ckv,
               w_ukv=w_ukv, g_sgu=g_sgu, w_sgu=w_sgu, b_sgu=b_sgu, w_out=w_out, g_ffn=g_ffn, w_ffn_in=w_ffn_in,
               ffn_conv_w=ffn_conv_w, ffn_conv_b=ffn_conv_b, w_ffn_out=w_ffn_out, g_final=g_final)
    sh = _prep_shared(inp)
    x_prompt = np.asarray(x_prompt, np.float32); x_sample = np.asarray(x_sample, np.float32)
    cnk = np.asarray(cache_na_k, np.float32); cnv = np.asarray(cache_na_v, np.float32)
    cckv = np.asarray(cache_mla_ckv, np.float32); ckr = np.asarray(cache_mla_krope, np.float32)
    c = np.asarray(c, np.float32); c_ctx = np.asarray(c_ctx, np.float32)
    in_maps = []
    for core in range(8):
        b = core // 2
        xp = x_prompt[2 * core:2 * core + 2].reshape(512, D)
        xT = np.ascontiguousarray(np.concatenate([x_sample[b], xp], 0).T)
        cond = np.stack([c[b], c_ctx], 0)
        m = dict(sh)
        m["xT"] = xT
        m["condT"] = np.ascontiguousarray(cond.reshape(2, 8, 128).transpose(2, 1, 0))
        m["c_na_kT"] = np.ascontiguousarray(cnk[b].reshape(L, 256, 256).transpose(0, 2, 1))
        m["c_na_v"] = np.ascontiguousarray(cnv[b].reshape(L, 256, 256))
        m["c_ckvT"] = np.ascontiguousarray(cckv[b].transpose(0, 2, 1))
        m["c_krT"] = np.ascontiguousarray(ckr[b].transpose(0, 2, 1))
        in_maps.append(m)
    key = (_depth, _stop)
    if key not in _NC_CACHE:
        _NC_CACHE[key] = build_nc(_depth, _stop)
    nc = _NC_CACHE[key]
    if _cores is not None:
        in_maps = in_maps[:_cores]
    res = run_bass_kernel_spmd(nc, in_maps, core_ids=list(range(len(in_maps))), **({"trace": True} if _trace else {}))
    R = res.results
    if _stop is not None:
        return R
    y_prompt = np.zeros((16, 256, D), np.float32)
    y_sample = np.zeros((4, NS, D), np.float32)
    nk = np.zeros((16, L, 256, 4, 64), np.float32)
    nv = np.zeros((16, L, 256, 4, 64), np.float32)
    nckv = np.zeros((16, L, 256, 256), np.float32)
    nkr = np.zeros((16, L, 256, 32), np.float32)
    for core in range(8):
        r = R[core]
        yT = r["yT"]
        if core % 2 == 0:
            y_sample[core // 2] = yT[:, 0:NS].T
        for i in range(2):
            sidx = 2 * core + i
            cs = slice(256 * i, 256 * (i + 1))
            y_prompt[sidx] = yT[:, NS + 256 * i:NS + 256 * (i + 1)].T
            nk[sidx] = r["o_kT"][:, :, cs].transpose(0, 2, 1).reshape(L, 256, 4, 64)
            nv[sidx] = r["o_v"][:, cs, :].reshape(L, 256, 4, 64)
            nckv[sidx] = r["o_ckvT"][:, :, cs].transpose(0, 2, 1)
            nkr[sidx] = r["o_krT"][:, :, cs].transpose(0, 2, 1)
    if _trace:
        kernel.last_res = res
    return (y_prompt, y_sample, nk, nv, nckv, nkr)
```
